# Optimizing a Trainium2 kernel written in Bass

```python
import math
import jax, jax.numpy as jnp
from jax import lax
import numpy as np

D_MODEL = 1024
BATCH = 32
SEQ = 2048
DEPTH = 4
DEC_BATCH = 2
DEC_SEQ = 16384
PAST_LEN = 128

GRID_W = 64
Q_BLOCK = 128
D_MIX = D_MODEL
A_WIDTH = D_MIX // 2
A_V_DIM = 128
A_HEAD_DIM = A_V_DIM // 2
A_HEADS = A_WIDTH // A_V_DIM
B_WIDTH = D_MIX - A_WIDTH
B_HEAD_DIM = 64
B_HEADS = B_WIDTH // B_HEAD_DIM
B_KV_HEADS = 2
B_GROUP = B_HEADS // B_KV_HEADS
NUM_BUCKETS = 32
MAX_DISTANCE = 128
ROPE_THETA = 10000.0
EPS = 1e-6
SPLIT_WIDTHS = (A_HEADS * 2 * A_HEAD_DIM, A_HEADS * 2 * A_HEAD_DIM, A_HEADS * A_V_DIM, A_WIDTH,
                B_HEADS * B_HEAD_DIM, B_KV_HEADS * B_HEAD_DIM, B_KV_HEADS * B_HEAD_DIM, B_WIDTH)
D_IN = sum(SPLIT_WIDTHS)
SPLIT_POINTS = tuple(sum(SPLIT_WIDTHS[:i + 1]) for i in range(len(SPLIT_WIDTHS) - 1))

kernel_name = "hymba_diffattn_gqa_axial_encoder"


def rms_norm(x, g):
    xf = x.astype(jnp.float32)
    y = xf * lax.rsqrt(jnp.mean(xf * xf, axis=-1, keepdims=True) + EPS)
    return (y * g.astype(jnp.float32)).astype(x.dtype)


def t5_bucket(rel):
    half = NUM_BUCKETS // 2
    max_exact = half // 2
    ret = jnp.where(rel > 0, half, 0)
    n = jnp.abs(rel)
    nf = jnp.maximum(n, 1).astype(jnp.float32)
    large = max_exact + (jnp.log(nf / max_exact) / math.log(MAX_DISTANCE / max_exact)
                         * (half - max_exact)).astype(jnp.int32)
    large = jnp.minimum(large, half - 1)
    return ret + jnp.where(n < max_exact, n, large)


def relative_bias_by_offset(rel_table, seq_len):
    offsets = jnp.arange(-(seq_len - 1), seq_len, dtype=jnp.int32)
    return rel_table[t5_bucket(offsets)].T.astype(jnp.float32)


def axial_rope_tables(seq_len):
    rows = seq_len // GRID_W
    row = jnp.repeat(jnp.arange(rows), GRID_W).astype(jnp.float32)
    col = jnp.tile(jnp.arange(GRID_W), rows).astype(jnp.float32)
    axis_dim = B_HEAD_DIM // 2
    inv_freq = ROPE_THETA ** (-jnp.arange(0, axis_dim, 2, dtype=jnp.float32) / axis_dim)
    ang_r = row[:, None] * inv_freq[None, :]
    ang_c = col[:, None] * inv_freq[None, :]
    ang = jnp.concatenate([ang_r, ang_r, ang_c, ang_c], axis=-1)
    return jnp.cos(ang), jnp.sin(ang)


def apply_axial_rope(x, cos, sin):
    xf = x.astype(jnp.float32)
    x1, x2, x3, x4 = jnp.split(xf, 4, axis=-1)
    rot = jnp.concatenate([-x2, x1, -x4, x3], axis=-1)
    return (xf * cos[:, None, :] + rot * sin[:, None, :]).astype(x.dtype)


def to_blocks(t):
    b, s = t.shape[:2]
    return jnp.moveaxis(t.reshape((b, s // Q_BLOCK, Q_BLOCK) + t.shape[2:]), 1, 0)


def from_blocks(t):
    nb, b, q = t.shape[:3]
    return jnp.moveaxis(t, 0, 1).reshape((b, nb * q) + t.shape[3:])


def diff_attention(q, k, v, lam, bias_off):
    s = q.shape[1]
    scale = A_HEAD_DIM ** -0.5
    k_pos = jnp.arange(s, dtype=jnp.int32)

    def block(args):
        qb, start = args
        q_pos = start + jnp.arange(Q_BLOCK, dtype=jnp.int32)
        idx = k_pos[None, :] - q_pos[:, None] + (s - 1)
        bias = bias_off[:, idx]
        logits = jnp.einsum('bqhcd,bkhcd->bhcqk', qb, k).astype(jnp.float32) * scale \
            + bias[None, :, None]
        p = jax.nn.softmax(logits, axis=-1)
        w = p[:, :, 0] - lam * p[:, :, 1]
        return jnp.einsum('bhqk,bkhe->bqhe', w.astype(v.dtype), v)

    starts = jnp.arange(s // Q_BLOCK, dtype=jnp.int32) * Q_BLOCK
    out = lax.map(block, (to_blocks(q), starts))
    return from_blocks(out)


def gqa_attention(q, k, v):
    b, s = q.shape[:2]
    scale = B_HEAD_DIM ** -0.5
    qg = q.reshape(b, s, B_KV_HEADS, B_GROUP, B_HEAD_DIM)

    def block(qb):
        logits = jnp.einsum('bqngd,bknd->bngqk', qb, k).astype(jnp.float32) * scale
        p = jax.nn.softmax(logits, axis=-1)
        return jnp.einsum('bngqk,bknd->bqngd', p.astype(v.dtype), v)

    out = lax.map(block, to_blocks(qg))
    return from_blocks(out).reshape(b, s, B_HEADS * B_HEAD_DIM)


def encoder_layer(x, c_act, layer_idx, norm_g, w_ada, b_ada, w_in, lam_q1, lam_k1, lam_q2, lam_k2,
                  subln_g, q_norm_g, k_norm_g, w_out, bias_off, cos, sin):
    b, s, _ = x.shape
    mod = c_act @ w_ada + b_ada
    shift, scale, gate = jnp.split(mod[:, None, :], 3, axis=-1)
    h = rms_norm(x, norm_g) * (1 + scale) + shift
    proj = h @ w_in
    qa, ka, va, ga, qb, kb, vb, gb = jnp.split(proj, SPLIT_POINTS, axis=-1)

    lam_init = 0.8 - 0.6 * math.exp(-0.3 * layer_idx)
    lam = (jnp.exp(jnp.sum(lam_q1.astype(jnp.float32) * lam_k1.astype(jnp.float32)))
           - jnp.exp(jnp.sum(lam_q2.astype(jnp.float32) * lam_k2.astype(jnp.float32))) + lam_init)
    oa = diff_attention(qa.reshape(b, s, A_HEADS, 2, A_HEAD_DIM),
                        ka.reshape(b, s, A_HEADS, 2, A_HEAD_DIM),
                        va.reshape(b, s, A_HEADS, A_V_DIM), lam, bias_off)
    oa = (rms_norm(oa, subln_g) * (1 - lam_init)).reshape(b, s, A_WIDTH)

    qb = apply_axial_rope(rms_norm(qb.reshape(b, s, B_HEADS, B_HEAD_DIM), q_norm_g), cos, sin)
    kb = apply_axial_rope(rms_norm(kb.reshape(b, s, B_KV_HEADS, B_HEAD_DIM), k_norm_g), cos, sin)
    ob = gqa_attention(qb, kb, vb.reshape(b, s, B_KV_HEADS, B_HEAD_DIM))

    mixed = jnp.concatenate([oa * jax.nn.silu(ga), ob * jax.nn.silu(gb)], axis=-1)
    return x + gate * (mixed @ w_out)


def run_trunk(x, c, rel_table, norm_g, w_ada, b_ada, w_in, lam_q1, lam_k1, lam_q2, lam_k2,
              subln_g, q_norm_g, k_norm_g, w_out, final_g):
    s = x.shape[1]
    bias_off = relative_bias_by_offset(rel_table, s)
    cos, sin = axial_rope_tables(s)
    c_act = jax.nn.silu(c)
    for l in range(DEPTH):
        x = encoder_layer(x, c_act, l, norm_g[l], w_ada[l], b_ada[l], w_in[l], lam_q1[l], lam_k1[l],
                          lam_q2[l], lam_k2[l], subln_g[l], q_norm_g[l], k_norm_g[l], w_out[l],
                          bias_off, cos, sin)
    return rms_norm(x, final_g)


def setup_inputs(seed: int = 0) -> dict:
    key = jax.random.key(seed)
    ks = jax.random.split(key, 20)
    f32 = jnp.float32
    nrm = lambda k, shape, s: jax.random.normal(k, shape, f32) * s
    return {
        "x_prompt": nrm(ks[0], (BATCH, SEQ, D_MODEL), 1.0),
        "x_sample": nrm(ks[1], (DEC_BATCH, DEC_SEQ, D_MODEL), 1.0),
        "c_prompt": nrm(ks[2], (BATCH, D_MODEL), 1.0),
        "c_sample": nrm(ks[3], (DEC_BATCH, D_MODEL), 1.0),
        "rel_table": nrm(ks[4], (NUM_BUCKETS, A_HEADS), 0.5),
        "norm_g": 1.0 + nrm(ks[5], (DEPTH, D_MODEL), 0.02),
        "w_ada": nrm(ks[6], (DEPTH, D_MODEL, 3 * D_MODEL), 0.5 * D_MODEL ** -0.5),
        "b_ada": nrm(ks[7], (DEPTH, 3 * D_MODEL), 0.02),
        "w_in": nrm(ks[8], (DEPTH, D_MODEL, D_IN), D_MODEL ** -0.5),
        "lam_q1": nrm(ks[9], (DEPTH, A_HEAD_DIM), 0.1),
        "lam_k1": nrm(ks[10], (DEPTH, A_HEAD_DIM), 0.1),
        "lam_q2": nrm(ks[11], (DEPTH, A_HEAD_DIM), 0.1),
        "lam_k2": nrm(ks[12], (DEPTH, A_HEAD_DIM), 0.1),
        "subln_g": 1.0 + nrm(ks[13], (DEPTH, A_V_DIM), 0.02),
        "q_norm_g": 1.0 + nrm(ks[14], (DEPTH, B_HEAD_DIM), 0.02),
        "k_norm_g": 1.0 + nrm(ks[15], (DEPTH, B_HEAD_DIM), 0.02),
        "w_out": nrm(ks[16], (DEPTH, D_MIX, D_MODEL), D_MIX ** -0.5),
        "final_g": 1.0 + nrm(ks[17], (D_MODEL,), 0.02),
    }


def reference(x_prompt, x_sample, c_prompt, c_sample, rel_table, norm_g, w_ada, b_ada, w_in,
              lam_q1, lam_k1, lam_q2, lam_k2, subln_g, q_norm_g, k_norm_g, w_out, final_g):
    y_prompt = run_trunk(x_prompt, c_prompt, rel_table, norm_g, w_ada, b_ada, w_in, lam_q1, lam_k1,
                         lam_q2, lam_k2, subln_g, q_norm_g, k_norm_g, w_out, final_g)
    y_sample = run_trunk(x_sample, c_sample, rel_table, norm_g, w_ada, b_ada, w_in, lam_q1, lam_k1,
                         lam_q2, lam_k2, subln_g, q_norm_g, k_norm_g, w_out, final_g)
    return (y_prompt, y_sample)
```

```python
import math
from contextlib import ExitStack

import numpy as np
import ml_dtypes

import concourse.bass as bass
import concourse.mybir as mybir
from concourse.bass_utils import run_bass_kernel_spmd

F32 = mybir.dt.float32
BF16 = mybir.dt.bfloat16
AF = mybir.ActivationFunctionType
ALU = mybir.AluOpType
AX = mybir.AxisListType

D = 1024
EPS = 1e-6
NQ7 = 22 * 128
WCOLS = NQ7 + 640
LP_M = 1280
LS_M = 2816
LM = LP_M + LS_M


class Cfg:
    def __init__(self, L=4, NP=4, SP=2048, NJ=8, stop=9, noag=False, debug=False):
        self.debug = debug
        self.L, self.NP, self.SP, self.NJ = L, NP, SP, NJ
        self.stop = stop
        self.noag = noag


class Buf:
    __slots__ = ("name", "w", "r")

    def __init__(self, name):
        self.name = name
        self.w = None
        self.r = {}


class Rec:
    ENG = ("pe", "act", "dve", "pool", "sp")

    def __init__(self, nc):
        self.nc = nc
        self.stream = {e: [] for e in self.ENG}
        self.sem = {e: nc.alloc_semaphore("s_" + e) for e in self.ENG}
        self.cnt = {e: 0 for e in self.ENG}
        self.waited = {e: {} for e in self.ENG}
        self.dsem = {}

    def _collect(self, eng, reads, writes, extra):
        evs = []
        for b in reads:
            if b.w is not None:
                evs.append((b.w, "raw"))
        for b in writes:
            for ev in b.r.values():
                evs.append((ev, "war"))
            if b.w is not None:
                evs.append((b.w, "waw"))
        for ev in extra:
            if ev is not None:
                evs.append((ev, "raw"))
        out = []
        for (ev, kind) in evs:
            h, val, key = ev
            if key == eng:
                if eng == "pe" or kind != "raw":
                    continue
            if self.waited[eng].get(key, 0) >= val:
                continue
            self.waited[eng][key] = val
            out.append((h, val))
        return out

    def op(self, eng, fn, reads=(), writes=(), extra=()):
        wl = self._collect(eng, reads, writes, extra)
        self.cnt[eng] += 1
        ev = (self.sem[eng], self.cnt[eng], eng)
        self.stream[eng].append((wl, fn, (self.sem[eng], 1)))
        for b in reads:
            b.r[eng] = ev
        for b in writes:
            b.w = ev
            b.r = {}
        return ev

    def dma(self, q, fn, semname, reads=(), writes=(), extra=(), inc=16):
        wl = self._collect(q, reads, writes, extra)
        key = "d_" + semname
        if semname not in self.dsem:
            self.dsem[semname] = [self.nc.alloc_semaphore(key), 0]
        d = self.dsem[semname]
        d[1] += inc
        ev = (d[0], d[1], key)
        self.stream[q].append((wl, fn, (d[0], inc)))
        for b in reads:
            b.r[key] = ev
        for b in writes:
            b.w = ev
            b.r = {}
        return ev

    def barrier(self):
        targets = []
        for e in self.ENG:
            if self.cnt[e] > 0:
                targets.append((self.sem[e], self.cnt[e], e))
        for name, d in self.dsem.items():
            if d[1] > 0:
                targets.append((d[0], d[1], "d_" + name))
        for e in self.ENG:
            wl = []
            for (h, val, key) in targets:
                if key == e:
                    continue
                if self.waited[e].get(key, 0) >= val:
                    continue
                self.waited[e][key] = val
                wl.append((h, val))
            if wl:
                self.stream[e].append((wl, None, None))

    def replay(self, eng, e):
        for (wl, fn, inc) in self.stream[eng]:
            for (h, val) in wl:
                e.wait_ge(h, val)
            if fn is not None:
                ins = fn(e)
                if inc is not None:
                    ins.then_inc(inc[0], inc[1])


def dap(handle, offset, dims):
    return bass.AP(handle, offset, [list(d) for d in dims])


def build_program(cfg):
    L, NP, SP, NJ = cfg.L, cfg.NP, cfg.SP, cfg.NJ
    NSEQ = 1 + NP
    TS = NJ * 512
    TP = NP * SP
    T = TS + TP
    NCH = T // 512
    SS = 4 * TS
    QP = SP // 512
    KP = SP // 128
    KS = SS // 128
    KMAX = max(KP, KS)
    SMAX = max(SP, SS)

    nc = bass.Bass("TRN2", target_bir_lowering=False)

    def din(name, shape, dt=F32):
        return nc.dram_tensor(name, list(shape), dt, kind="ExternalInput")

    def dscr(name, shape, dt):
        if cfg.debug and name in ("qT_d", "gT_d", "kTp_d", "vp_d", "mixT_d", "M_d", "gate_d", "xbuf"):
            return nc.dram_tensor(name, list(shape), dt, kind="ExternalOutput")
        return nc.dram_tensor(name, list(shape), dt)

    x_s = din("x_s", [TS, D])
    x_p = din("x_p", [TP, D])
    cT_d = din("cT", [128, 8 * NSEQ])
    w_ada = din("w_ada", [L, D, 3 * D])
    b_adaT = din("b_adaT", [128, L * 16])
    b_gate = din("b_gate", [L, D])
    w_in = din("w_in", [L, D, 3328])
    w_out = din("w_out", [L, D, D])
    ngT_d = din("ngT", [128, L * 8])
    smallp_d = din("smallp", [128, 3 * L + 8])
    lamv_d = din("lamv", [128, L * 256])
    fg_d = din("fgbc", [128, D])
    table_d = din("table", [32, 4])
    oh_d = din("onehot", [32, LM])
    cs_d = din("cstab", [128, 2, T])
    consts_d = din("consts", [128, 640])
    y_s = nc.dram_tensor("y_s", [TS, D], F32, kind="ExternalOutput")
    y_p = nc.dram_tensor("y_p", [TP, D], F32, kind="ExternalOutput")

    xbuf = dscr("xbuf", [T, D], F32)
    qT_d = dscr("qT_d", [8, 128, T], BF16)
    gT_d = dscr("gT_d", [8, 128, T], BF16)
    kTp_d = dscr("kTp_d", [6, 128, TP], BF16)
    vp_d = dscr("vp_d", [TP, 640], BF16)
    kTc_d = [[dscr(f"kTc_d{i}_{k}", [128, TS], BF16) for k in range(6)] for i in range(2)]
    kTg_d = [[dscr(f"kTg_d{i}_{k}", [4 * 128, TS], BF16) for k in range(6)] for i in range(2)]
    vc_d = [[dscr(f"vc_d{i}_{j}", [512, 640], BF16) for j in range(NJ)] for i in range(2)]
    vg_d = [[dscr(f"vg_d{i}_{j}", [4 * 512, 640], BF16) for j in range(NJ)] for i in range(2)]
    mixT_d = dscr("mixT_d", [D, T], BF16)
    M_d = dscr("M_d", [4, LM], BF16)
    gate_d = dscr("gate_d", [L * NSEQ, 128, D], F32)

    R = Rec(nc)

    def chunk_info(c):
        if c < NJ:
            return 0, True, c
        pc = c - NJ
        return 1 + pc // QP, False, pc

    with ExitStack() as top:
        uid = [0]

        def sb(name, shape, dt, st=top):
            uid[0] += 1
            return st.enter_context(nc.sbuf_tensor(f"{name}_u{uid[0]}", list(shape), dt))

        def ps(name, shape, dt, st):
            uid[0] += 1
            return st.enter_context(nc.psum_tensor(f"{name}_u{uid[0]}", list(shape), dt))

        consts_b = sb("consts_b", [128, 640], BF16)
        ident = consts_b[:, 0:128]
        Jm = consts_b[:, 128:256]
        ones_b = consts_b[:, 256:384]
        bones = consts_b[:, 384:512]
        rrot = consts_b[:, 512:640]
        ngT = sb("ngT_s", [128, L * 8], F32)
        smallp = sb("smallp_s", [128, 3 * L + 8], F32)
        far = smallp[:, 3 * L:3 * L + 8]
        neglam = sb("neglam", [128, L], F32)
        gsub = sb("gsub", [128, L], F32)
        qg2 = sb("qg2", [128, L], F32)
        AT = sb("AT", [128, L * NSEQ * 8], F32)
        BT = sb("BT", [128, L * NSEQ * 8], F32)
        B_const = Buf("consts")

        with ExitStack() as st:
            consts_f = sb("consts_f", [128, 640], F32, st)
            lamv = sb("lamv_s", [128, L * 256], F32, st)
            lamt = sb("lamt", [128, 64], F32, st)
            lams = sb("lams", [128, 2 * L], F32, st)
            lame = sb("lame", [128, 2 * L], F32, st)
            cT = sb("cT_s", [128, 8 * NSEQ], F32, st)
            cact = sb("cact", [128, 8 * NSEQ], F32, st)
            cact_b = sb("cact_b", [128, 8 * NSEQ], BF16, st)
            ones_f = sb("ones_f", [128, 128], F32, st)
            crep = sb("crep", [128, NSEQ * 8, 128], BF16, st)
            b_adaT_s = sb("b_adaT_s", [128, L * 16], F32, st)
            modT = sb("modT", [128, L * 16 * NSEQ], F32, st)
            bg = sb("bg", [128, D], F32, st)
            wst = [sb(f"wast{i}", [128, 8, 512], F32, st) for i in range(2)]
            wab = [sb(f"wab{i}", [128, 8, 512], BF16, st) for i in range(2)]
            gsb = [sb(f"gsb{i}", [128, 512], F32, st) for i in range(2)]
            tab_f = sb("tab_f", [32, 4], F32, st)
            tab_b = sb("tab_b", [32, 4], BF16, st)
            oh_f = sb("oh_f", [32, LM], F32, st)
            oh_b = sb("oh_b", [32, LM], BF16, st)
            M_s = sb("M_s", [4, LM], BF16, st)
            tmp8 = sb("tmp8", [128, 8], F32, st)
            pmod = ps("pmod", [128, 512], F32, st)
            pgate = [ps(f"pgate{i}", [128, 512], F32, st) for i in range(2)]
            pM = [ps(f"pM{i}", [4, 512], F32, st) for i in range(2)]

            Bc_f, Blamv, BcT, Bbada, Btab, Boh = (Buf(n) for n in ("cf", "lamv", "cT", "bada", "tab", "oh"))
            Bsmall, BngT = Buf("small"), Buf("ngT")
            R.dma("sp", lambda e: e.dma_start(out=consts_f[:], in_=consts_d.ap()), "cf", writes=[Bc_f])
            R.dma("sp", lambda e: e.dma_start(out=lamv[:], in_=lamv_d.ap()), "lamv", writes=[Blamv])
            R.dma("sp", lambda e: e.dma_start(out=cT[:], in_=cT_d.ap()), "cT", writes=[BcT])
            R.dma("sp", lambda e: e.dma_start(out=b_adaT_s[:], in_=b_adaT.ap()), "bada", writes=[Bbada])
            R.dma("sp", lambda e: e.dma_start(out=tab_f[:], in_=table_d.ap()), "tab", writes=[Btab])
            R.dma("sp", lambda e: e.dma_start(out=oh_f[:], in_=oh_d.ap()), "oh", writes=[Boh])
            R.dma("sp", lambda e: e.dma_start(out=smallp[:], in_=smallp_d.ap()), "small", writes=[Bsmall])
            R.dma("sp", lambda e: e.dma_start(out=ngT[:], in_=ngT_d.ap()), "ngT", writes=[BngT])

            R.op("dve", lambda e: e.tensor_copy(out=consts_b[:], in_=consts_f[:]), reads=[Bc_f], writes=[B_const])
            Bones_f = Buf("ones_f")
            R.op("dve", lambda e: e.memset(ones_f[:], 1.0), writes=[Bones_f])
            Blamt, Blams, Blame, Bneglam = Buf("lamt"), Buf("lams"), Buf("lame"), Buf("neglam")
            for l in range(L):
                for k in range(2):
                    a0 = l * 256 + k * 128
                    R.op("dve", lambda e, a0=a0: e.tensor_tensor(out=lamt[:], in0=lamv[:, a0:a0 + 64],
                                                                 in1=lamv[:, a0 + 64:a0 + 128], op=ALU.mult),
                         reads=[Blamv], writes=[Blamt])
                    R.op("dve", lambda e, l=l, k=k: e.reduce_sum(out=lams[:, 2 * l + k:2 * l + k + 1], in_=lamt[:],
                                                                 axis=AX.X),
                         reads=[Blamt], writes=[Blams])
            R.op("act", lambda e: e.activation(out=lame[:], in_=lams[:], func=AF.Exp), reads=[Blams], writes=[Blame])
            for l in range(L):
                lam_init = 0.8 - 0.6 * math.exp(-0.3 * l)
                R.op("dve", lambda e, l=l: e.tensor_tensor(out=neglam[:, l:l + 1], in0=lame[:, 2 * l + 1:2 * l + 2],
                                                           in1=lame[:, 2 * l:2 * l + 1], op=ALU.subtract),
                     reads=[Blame], writes=[Bneglam])
                R.op("dve", lambda e, l=l, li=lam_init: e.tensor_scalar_add(out=neglam[:, l:l + 1],
                                                                           in0=neglam[:, l:l + 1], scalar1=-li),
                     reads=[Bneglam], writes=[Bneglam])
                R.op("dve", lambda e, l=l, li=lam_init: e.tensor_scalar_mul(out=gsub[:, l:l + 1],
                                                                           in0=smallp[:, l:l + 1], scalar1=1.0 - li),
                     reads=[Bsmall], writes=[B_const])
                R.op("dve", lambda e, l=l: e.tensor_scalar_mul(out=qg2[:, l:l + 1],
                                                               in0=smallp[:, L + l:L + l + 1], scalar1=0.125),
                     reads=[Bsmall], writes=[B_const])
            Bcact, Bcactb, Bcrep = Buf("cact"), Buf("cactb"), Buf("crep")
            R.op("act", lambda e: e.activation(out=cact[:], in_=cT[:], func=AF.Silu), reads=[BcT], writes=[Bcact])
            R.op("dve", lambda e: e.tensor_copy(out=cact_b[:], in_=cact[:]), reads=[Bcact], writes=[Bcactb])
            for s in range(NSEQ):
                for fc in range(8):
                    i = fc * NSEQ + s
                    R.op("dve", lambda e, i=i, s=s, fc=fc: e.tensor_scalar(
                        out=crep[:, s * 8 + fc, :], in0=ones_f[:], scalar1=cact[:, i:i + 1], scalar2=None,
                        op0=ALU.mult), reads=[Bcact, Bones_f], writes=[Bcrep])
            Btabb, Bohb, BMs = Buf("tabb"), Buf("ohb"), Buf("Ms")
            R.op("dve", lambda e: e.tensor_copy(out=tab_b[:], in_=tab_f[:]), reads=[Btab], writes=[Btabb])
            R.op("pool", lambda e: e.tensor_copy(out=oh_b[:], in_=oh_f[:]), reads=[Boh], writes=[Bohb])
            BpM = [Buf("pM0"), Buf("pM1")]
            for k in range(LM // 512):
                R.op("pe", lambda e, k=k: e.matmul(pM[k % 2][:, :], lhsT=tab_b[:, :], rhs=oh_b[:, k * 512:(k + 1) * 512],
                                                   start=True, stop=True),
                     reads=[Btabb, Bohb], writes=[BpM[k % 2]])
                R.op("dve", lambda e, k=k: e.tensor_copy(out=M_s[:, k * 512:(k + 1) * 512], in_=pM[k % 2][:, :]),
                     reads=[BpM[k % 2]], writes=[BMs])
            BMd = Buf("Md")
            R.dma("sp", lambda e: e.dma_start(out=M_d.ap(), in_=M_s[:]), "Mst", reads=[BMs], writes=[BMd])

            Bwst = [Buf("wst0"), Buf("wst1")]
            Bwab = [Buf("wab0"), Buf("wab1")]
            Bpmod, BmodT, Bbg = Buf("pmod"), Buf("modT"), Buf("bg")
            Bpg = [Buf("pg0"), Buf("pg1")]
            Bgsb = [Buf("gsb0"), Buf("gsb1")]
            k = 0
            gcount = 0
            for l in range(L):
                R.dma("sp", lambda e, l=l: e.dma_start(out=bg[:], in_=b_gate[l:l + 1, :].partition_broadcast(128)),
                      "bg", writes=[Bbg])
                for nb in range(6):
                    slot = k % 2
                    k += 1
                    src = dap(w_ada, l * D * 3 * D + nb * 512, [[3 * D, 128], [128 * 3 * D, 8], [1, 512]])
                    R.dma("sp", lambda e, slot=slot, src=src: e.dma_start(out=wst[slot][:], in_=src),
                          f"wast{slot}", writes=[Bwst[slot]])
                    R.op("pool", lambda e, slot=slot: e.tensor_copy(out=wab[slot][:], in_=wst[slot][:]),
                         reads=[Bwst[slot]], writes=[Bwab[slot]])
                    if nb < 4:
                        for cb in range(4):
                            blk = nb * 4 + cb
                            for fc in range(8):
                                R.op("pe", lambda e, slot=slot, cb=cb, fc=fc: e.matmul(
                                    pmod[:, 0:NSEQ], lhsT=wab[slot][:, fc, cb * 128:(cb + 1) * 128],
                                    rhs=cact_b[:, fc * NSEQ:(fc + 1) * NSEQ], start=(fc == 0), stop=(fc == 7)),
                                    reads=[Bwab[slot], Bcactb], writes=[Bpmod])
                            o0 = (l * 16 + blk) * NSEQ
                            R.op("dve", lambda e, o0=o0, l=l, blk=blk: e.tensor_scalar(
                                out=modT[:, o0:o0 + NSEQ], in0=pmod[:, 0:NSEQ],
                                scalar1=b_adaT_s[:, l * 16 + blk:l * 16 + blk + 1], scalar2=None, op0=ALU.add),
                                reads=[Bpmod, Bbada], writes=[BmodT])
                    else:
                        half = nb - 4
                        for s in range(NSEQ):
                            g = gcount % 2
                            gcount += 1
                            for fc in range(8):
                                R.op("pe", lambda e, g=g, s=s, fc=fc, slot=slot: e.matmul(
                                    pgate[g][:, :], lhsT=crep[:, s * 8 + fc, :], rhs=wab[slot][:, fc, :],
                                    start=(fc == 0), stop=(fc == 7)),
                                    reads=[Bwab[slot], Bcrep], writes=[Bpg[g]])
                            R.op("dve", lambda e, g=g, half=half: e.tensor_tensor(
                                out=gsb[g][:], in0=pgate[g][:, :], in1=bg[:, half * 512:(half + 1) * 512], op=ALU.add),
                                reads=[Bpg[g], Bbg], writes=[Bgsb[g]])
                            dst = dap(gate_d, (l * NSEQ + s) * 128 * D + half * 512, [[D, 128], [1, 512]])
                            R.dma("sp", lambda e, g=g, dst=dst: e.dma_start(out=dst, in_=gsb[g][:]),
                                  f"gsb{g}", reads=[Bgsb[g]])
                Btmp8 = Buf("tmp8")
                for s in range(NSEQ):
                    base = l * 16 * NSEQ
                    o = (l * NSEQ + s) * 8
                    sc0 = base + 8 * NSEQ + s
                    sh0 = base + s
                    R.op("dve", lambda e, sc0=sc0: e.tensor_scalar_add(
                        out=tmp8[:], in0=modT[:, sc0:sc0 + 7 * NSEQ + 1:NSEQ], scalar1=1.0),
                        reads=[BmodT], writes=[Btmp8])
                    R.op("dve", lambda e, o=o, l=l: e.tensor_tensor(
                        out=AT[:, o:o + 8], in0=tmp8[:], in1=ngT[:, l * 8:(l + 1) * 8], op=ALU.mult),
                        reads=[Btmp8, BngT], writes=[B_const])
                    R.op("dve", lambda e, o=o, sh0=sh0: e.tensor_copy(
                        out=BT[:, o:o + 8], in_=modT[:, sh0:sh0 + 7 * NSEQ + 1:NSEQ]),
                        reads=[BmodT], writes=[B_const])
            R.barrier()

        def P1(l, par, last):
            with ExitStack() as st:
                W_b = sb("W_b", [128, 8, WCOLS], BF16, st)
                wstage = [sb("wstage0", [128, 3328], F32, st)]
                xch = [sb(f"xch{i}", [128, 4, D], F32, st) for i in range(2)]
                junk = sb("junk", [128, D], F32, st)
                ssq = sb("ssq", [128, 4], F32, st)
                rstd = sb("rstd", [128, 4], F32, st)
                xs = [sb(f"xs{i}", [128, 4, D], BF16, st) for i in range(2)]
                hT = [sb(f"hT{i}", [128, 8, 512], BF16, st) for i in range(2)]
                NFM = 6
                fmo = [sb(f"fmo{i}", [128, 512], BF16, st) for i in range(NFM)]
                vout = [sb(f"vout{i}", [128, 640], BF16, st) for i in range(2)]
                cs = [sb(f"cs{i}", [128, 2, 512], F32, st) for i in range(2)]
                sqb = [sb(f"sqb{i}", [128, 512], BF16, st) for i in range(3)]
                rs = [sb(f"rs{i}", [128, 512], F32, st) for i in range(3)]
                qnf = [sb(f"qnf{i}", [128, 512], F32, st) for i in range(3)]
                qnb = [sb(f"qnb{i}", [128, 512], BF16, st) for i in range(3)]
                t2 = [sb(f"t2{i}", [128, 512], F32, st) for i in range(3)]
                ptr = [ps(f"ptr{i}", [128, 1024], BF16, st) for i in range(2)]
                pmm = [ps(f"pmm{i}", [128, 512], F32, st) for i in range(4)]
                paux = [ps(f"paux{i}", [128, 512], F32, st) for i in range(2)]

                BW = Buf("W_b")
                Bwstage = [Buf("wstage0")]
                Bxch = [Buf("xch0"), Buf("xch1")]
                Bjunk, Bssq, Brstd = Buf("junk"), Buf("ssq"), Buf("rstd")
                Bxs = [Buf("xs0"), Buf("xs1")]
                BhT = [Buf("hT0"), Buf("hT1")]
                Bfmo = [Buf(f"fmo{i}") for i in range(NFM)]
                Bvout = [Buf("vout0"), Buf("vout1")]
                Bcs = [Buf("cs0"), Buf("cs1")]
                Bsqb = [Buf(f"sqb{i}") for i in range(3)]
                Brs = [Buf(f"rs{i}") for i in range(3)]
                Bqnf = [Buf(f"qnf{i}") for i in range(3)]
                Bqnb = [Buf(f"qnb{i}") for i in range(3)]
                Bt2 = [Buf(f"t2{i}") for i in range(3)]
                Bptr = [Buf("ptr0"), Buf("ptr1")]
                Bpmm = [Buf(f"pmm{i}") for i in range(4)]
                Bpaux = [Buf("paux0"), Buf("paux1")]
                BkTc, Bvc = Buf("kTc"), Buf("vc")

                casts = [
                    (0, 0, 512), (512, 512, 512), (1024, 1536, 512), (1536, 2048, 512),
                    (2048, 2560, 64), (2112, 2560, 64), (2176, 2624, 64), (2240, 2624, 64),
                    (2304, 2816, 512), (2816, 1024, 512), (3328, 2688, 128)]
                for fc in range(8):
                    slot = 0
                    src = dap(w_in, l * D * 3328 + fc * 128 * 3328, [[3328, 128], [1, 3328]])
                    R.dma("sp", lambda e, slot=slot, src=src: e.dma_start(out=wstage[slot][:], in_=src),
                          f"wstage{slot}", writes=[Bwstage[slot]])
                    for ci, (dc, sc, w) in enumerate(casts):
                        eng = "dve" if ci % 2 == 0 else "pool"
                        R.op(eng, lambda e, slot=slot, fc=fc, dc=dc, sc=sc, w=w: e.tensor_copy(
                            out=W_b[:, fc, dc:dc + w], in_=wstage[slot][:, sc:sc + w]),
                            reads=[Bwstage[slot]], writes=[BW])

                cnt = {"mm": 0, "fm": 0, "aux": 0}

                def emit_xload(c):
                    s_, is_s_, loc_ = chunk_info(c)
                    cslot_ = c % 2
                    r0_ = c * 512
                    if l == 0:
                        xsrc = dap(x_s, loc_ * 512 * D, [[D, 128], [128 * D, 4], [1, D]]) if is_s_ else \
                            dap(x_p, loc_ * 512 * D, [[D, 128], [128 * D, 4], [1, D]])
                    else:
                        xsrc = dap(xbuf, r0_ * D, [[D, 128], [128 * D, 4], [1, D]])
                    R.dma("sp", lambda e: e.dma_start(out=xch[cslot_][:], in_=xsrc),
                          f"xch{cslot_}", writes=[Bxch[cslot_]])

                def emit_csload(c):
                    cslot_ = c % 2
                    cssrc = dap(cs_d, c * 512, [[2 * T, 128], [T, 2], [1, 512]])
                    R.dma("sp", lambda e: e.dma_start(out=cs[cslot_][:], in_=cssrc),
                          f"cs{cslot_}", writes=[Bcs[cslot_]])

                def fe_act(c):
                    cslot = c % 2
                    for t in range(4):
                        R.op("act", lambda e, t=t: e.activation(
                            out=junk[:], in_=xch[cslot][:, t, :], func=AF.Square, accum_out=ssq[:, t:t + 1]),
                            reads=[Bxch[cslot]], writes=[Bjunk, Bssq])
                    R.op("act", lambda e: e.activation(out=rstd[:], in_=ssq[:], func=AF.Ln, scale=1.0 / D, bias=EPS),
                         reads=[Bssq], writes=[Brstd])
                    R.op("act", lambda e: e.activation(out=rstd[:], in_=rstd[:], func=AF.Exp, scale=-0.5),
                         reads=[Brstd], writes=[Brstd])
                    for t in range(4):
                        R.op("act", lambda e, t=t: e.activation(
                            out=xs[cslot][:, t, :], in_=xch[cslot][:, t, :], func=AF.Copy, scale=rstd[:, t:t + 1]),
                            reads=[Bxch[cslot], Brstd], writes=[Bxs[cslot]])

                def fe_pe(c):
                    cslot = c % 2
                    s_, is_s_, loc_ = chunk_info(c)
                    mo = (l * NSEQ + s_) * 8
                    for fc in range(8):
                        tp = fc % 2
                        for t in range(4):
                            R.op("pe", lambda e, tp=tp, t=t, fc=fc: e.transpose(
                                ptr[tp][:, t * 128:(t + 1) * 128], xs[cslot][:, t, fc * 128:(fc + 1) * 128], ident),
                                reads=[Bxs[cslot], B_const], writes=[Bptr[tp]])
                        R.op("dve", lambda e, tp=tp, fc=fc: e.tensor_scalar(
                            out=hT[cslot][:, fc, :], in0=ptr[tp][:, 0:512], scalar1=AT[:, mo + fc:mo + fc + 1],
                            scalar2=BT[:, mo + fc:mo + fc + 1], op0=ALU.mult, op1=ALU.add),
                            reads=[Bptr[tp], B_const], writes=[BhT[cslot]])

                emit_xload(0)
                if NCH > 1:
                    emit_xload(1)
                emit_csload(0)
                fe_act(0)
                fe_pe(0)

                def chunk_body(c):
                    s, is_s, loc = chunk_info(c)
                    cslot = c % 2
                    r0 = c * 512
                    if c + 2 < NCH:
                        emit_xload(c + 2)
                    if c + 1 < NCH:
                        emit_csload(c + 1)

                    def next_pm():
                        k = cnt["mm"] % 4
                        cnt["mm"] += 1
                        return k

                    def next_fm():
                        k = cnt["fm"] % NFM
                        cnt["fm"] += 1
                        return k

                    def next_aux():
                        k = cnt["aux"] % 2
                        cnt["aux"] += 1
                        return k

                    def fm_matmul(blk, pmi):
                        for fc in range(8):
                            R.op("pe", lambda e, fc=fc, blk=blk, pmi=pmi, cslot=cslot: e.matmul(
                                pmm[pmi][:, :], lhsT=W_b[:, fc, blk * 128:(blk + 1) * 128], rhs=hT[cslot][:, fc, :],
                                start=(fc == 0), stop=(fc == 7)),
                                reads=[BW, BhT[cslot]], writes=[Bpmm[pmi]])

                    def kdst(kunit):
                        if is_s:
                            return dap(kTc_d[par][kunit], loc * 512, [[TS, 128], [1, 512]]), BkTc
                        return dap(kTp_d, kunit * 128 * TP + loc * 512, [[TP, 128], [1, 512]]), None

                    def store_fm(fi, dst, extra_w=None):
                        ws = [] if extra_w is None else [extra_w]
                        R.dma("sp", lambda e, fi=fi, dst=dst: e.dma_start(out=dst, in_=fmo[fi][:]),
                              f"fmo{fi}", reads=[Bfmo[fi]], writes=ws)

                    def do_qa(h):
                        pmi, fi = next_pm(), next_fm()
                        fm_matmul(h, pmi)
                        R.op("dve", lambda e: e.tensor_scalar_mul(out=fmo[fi][:], in0=pmm[pmi][:, :], scalar1=0.125),
                             reads=[Bpmm[pmi]], writes=[Bfmo[fi]])
                        store_fm(fi, dap(qT_d, h * 128 * T + r0, [[T, 128], [1, 512]]))

                    def do_ka(h):
                        pmi, fi = next_pm(), next_fm()
                        fm_matmul(4 + h, pmi)
                        R.op("act", lambda e: e.activation(out=fmo[fi][:], in_=pmm[pmi][:, :], func=AF.Copy),
                             reads=[Bpmm[pmi]], writes=[Bfmo[fi]])
                        dst, bw = kdst(h)
                        store_fm(fi, dst, bw)

                    def do_g(j):
                        pmi, fi = next_pm(), next_fm()
                        blk = 8 + j if j < 4 else 18 + (j - 4)
                        fm_matmul(blk, pmi)
                        R.op("act", lambda e: e.activation(out=fmo[fi][:], in_=pmm[pmi][:, :], func=AF.Silu),
                             reads=[Bpmm[pmi]], writes=[Bfmo[fi]])
                        store_fm(fi, dap(gT_d, j * 128 * T + r0, [[T, 128], [1, 512]]))

                    def do_v(t):
                        vs = (c * 4 + t) % 2
                        pmi, pmi2 = next_pm(), next_pm()
                        for fc in range(8):
                            R.op("pe", lambda e, fc=fc: e.matmul(
                                pmm[pmi][:, :], lhsT=hT[cslot][:, fc, t * 128:(t + 1) * 128],
                                rhs=W_b[:, fc, NQ7:NQ7 + 512], start=(fc == 0), stop=(fc == 7)),
                                reads=[BW, BhT[cslot]], writes=[Bpmm[pmi]])
                        for fc in range(8):
                            R.op("pe", lambda e, fc=fc: e.matmul(
                                pmm[pmi2][:, 0:128], lhsT=hT[cslot][:, fc, t * 128:(t + 1) * 128],
                                rhs=W_b[:, fc, NQ7 + 512:NQ7 + 640], start=(fc == 0), stop=(fc == 7)),
                                reads=[BW, BhT[cslot]], writes=[Bpmm[pmi2]])
                        R.op("dve", lambda e: e.tensor_copy(out=vout[vs][:, 0:512], in_=pmm[pmi][:, :]),
                             reads=[Bpmm[pmi]], writes=[Bvout[vs]])
                        R.op("dve", lambda e: e.tensor_copy(out=vout[vs][:, 512:640], in_=pmm[pmi2][:, 0:128]),
                             reads=[Bpmm[pmi2]], writes=[Bvout[vs]])
                        if is_s:
                            vdst = dap(vc_d[par][loc], t * 128 * 640, [[640, 128], [1, 640]])
                            R.dma("sp", lambda e: e.dma_start(out=vdst, in_=vout[vs][:]),
                                  f"vout{vs}", reads=[Bvout[vs]], writes=[Bvc])
                        else:
                            vdst = dap(vp_d, (loc * 512 + t * 128) * 640, [[640, 128], [1, 640]])
                            R.dma("sp", lambda e: e.dma_start(out=vdst, in_=vout[vs][:]),
                                  f"vout{vs}", reads=[Bvout[vs]])

                    chain_pm = {}

                    def stageA(j):
                        a = j % 3
                        pmi = next_pm()
                        chain_pm[j] = pmi
                        fm_matmul(12 + j, pmi)
                        R.op("act", lambda e: e.activation(out=sqb[a][:], in_=pmm[pmi][:, :], func=AF.Square),
                             reads=[Bpmm[pmi]], writes=[Bsqb[a]])

                    def stageB(j):
                        a = j % 3
                        pmi = chain_pm[j]
                        ax = next_aux()
                        R.op("pe", lambda e: e.matmul(paux[ax][:, :], lhsT=bones, rhs=sqb[a][:], start=True, stop=True),
                             reads=[Bsqb[a], B_const], writes=[Bpaux[ax]])
                        R.op("act", lambda e: e.activation(out=rs[a][:], in_=paux[ax][:, :], func=AF.Ln,
                                                           scale=1.0 / 64, bias=EPS),
                             reads=[Bpaux[ax]], writes=[Brs[a]])
                        R.op("act", lambda e: e.activation(out=rs[a][:], in_=rs[a][:], func=AF.Exp, scale=-0.5),
                             reads=[Brs[a]], writes=[Brs[a]])
                        gcol = qg2[:, l:l + 1] if j < 4 else smallp[:, 2 * L + l:2 * L + l + 1]
                        R.op("dve", lambda e: e.scalar_tensor_tensor(
                            out=qnf[a][:], in0=pmm[pmi][:, :], scalar=gcol, in1=rs[a][:], op0=ALU.mult, op1=ALU.mult),
                            reads=[Bpmm[pmi], Brs[a], B_const], writes=[Bqnf[a]])
                        R.op("dve", lambda e: e.tensor_copy(out=qnb[a][:], in_=qnf[a][:]),
                             reads=[Bqnf[a]], writes=[Bqnb[a]])

                    def stageC(j):
                        a = j % 3
                        fi = next_fm()
                        ax2 = next_aux()
                        R.op("pe", lambda e: e.matmul(paux[ax2][:, :], lhsT=rrot, rhs=qnb[a][:], start=True, stop=True),
                             reads=[Bqnb[a], B_const], writes=[Bpaux[ax2]])
                        R.op("dve", lambda e: e.tensor_tensor(out=t2[a][:], in0=paux[ax2][:, :], in1=cs[cslot][:, 1, :],
                                                              op=ALU.mult),
                             reads=[Bpaux[ax2], Bcs[cslot]], writes=[Bt2[a]])
                        R.op("dve", lambda e: e.tensor_tensor(out=qnf[a][:], in0=qnf[a][:], in1=cs[cslot][:, 0, :],
                                                              op=ALU.mult),
                             reads=[Bqnf[a], Bqnb[a], Bcs[cslot]], writes=[Bqnf[a]])
                        R.op("dve", lambda e: e.tensor_tensor(out=fmo[fi][:], in0=qnf[a][:], in1=t2[a][:], op=ALU.add),
                             reads=[Bqnf[a], Bt2[a]], writes=[Bfmo[fi]])
                        if j < 4:
                            store_fm(fi, dap(qT_d, (4 + j) * 128 * T + r0, [[T, 128], [1, 512]]))
                        else:
                            dst, bw = kdst(4 + (j - 4))
                            store_fm(fi, dst, bw)

                    fillers = [(1, do_qa, h) for h in range(4)] + [(1, do_ka, h) for h in range(4)] + \
                              [(1, do_g, j) for j in range(8)] + [(2, do_v, t) for t in range(4)]
                    tick = 0
                    while fillers or tick < 8:
                        if 0 <= tick - 1 < 6:
                            stageB(tick - 1)
                        if 0 <= tick - 2 < 6:
                            stageC(tick - 2)
                        budget = 2
                        while fillers and fillers[0][0] <= budget:
                            cost, fn, arg = fillers.pop(0)
                            budget -= cost
                            fn(arg)
                        if tick < 6:
                            stageA(tick)
                        budget = 1
                        while fillers and fillers[0][0] <= budget:
                            cost, fn, arg = fillers.pop(0)
                            budget -= cost
                            fn(arg)
                        if c + 1 < NCH:
                            if tick == 0:
                                fe_act(c + 1)
                            if tick == 4:
                                fe_pe(c + 1)
                        tick += 1
                    if c == NJ - 1 and not cfg.noag:
                        extra = []
                        for nm in [f"fmo{i}" for i in range(NFM)] + ["vout0", "vout1"]:
                            d = R.dsem.get(nm)
                            if d is not None:
                                extra.append((d[0], d[1], "d_" + nm))
                        prev_ev = None
                        pairs = [(kTc_d[par][k], kTg_d[par][k]) for k in range(6)] + \
                                [(vc_d[par][j], vg_d[par][j]) for j in range(NJ)]
                        for (src_t, dst_t) in pairs:
                            ex = list(extra) + ([prev_ev] if prev_ev is not None else [])
                            prev_ev = R.dma("pool", lambda e, src_t=src_t, dst_t=dst_t: e.collective_compute(
                                "AllGather", ALU.bypass, replica_groups=[[0, 1, 2, 3], [4, 5, 6, 7]],
                                ins=[src_t.ap()], outs=[dst_t.ap()]), "cc", extra=ex, inc=1)

                for c in range(NCH):
                    chunk_body(c)
                R.barrier()

        def P3(l, par, last):
            with ExitStack() as st:
                kbuf = [sb(f"kbuf{i}", [128, SMAX], BF16, st) for i in range(2)]
                vbuf = [sb(f"vbuf{i}", [128, KMAX, 128], BF16, st) for i in range(2)]
                Hbuf = [sb(f"Hbuf{i}", [128, 2688], BF16, st) for i in range(2)]
                NQB = 3
                qbuf = [sb(f"qbuf{i}", [128, 512], BF16, st) for i in range(NQB)]
                gbuf = [sb(f"gbuf{i}", [128, 512], BF16, st) for i in range(4)]
                NPT = 4
                pT = [sb(f"pT{i}", [128, 1024], BF16, st) for i in range(NPT)]
                zsum = [sb(f"zsum{i}", [128, 1024], BF16, st) for i in range(2)]
                Bzsum = [Buf("zsum0"), Buf("zsum1")]
                rzp = [sb(f"rzp{i}", [128, 512], F32, st) for i in range(2)]
                rz0f = sb("rz0f", [128, 512], F32, st)
                rz1f = sb("rz1f", [128, 512], F32, st)
                ob = [sb(f"ob{i}", [128, 512], F32, st) for i in range(2)]
                tb = [sb(f"tb{i}", [128, 512], F32, st) for i in range(2)]
                sq = [sb(f"sq{i}", [128, 512], BF16, st) for i in range(2)]
                rsd = [sb(f"rsd{i}", [128, 512], F32, st) for i in range(2)]
                mixo = [sb(f"mixo{i}", [128, 512], BF16, st) for i in range(2)]
                psS = [ps(f"psS{i}", [128, 1024], F32, st) for i in range(2)]
                psO0 = ps("psO0", [128, 512], F32, st)
                psO1 = ps("psO1", [128, 512], F32, st)
                psZ0 = ps("psZ0", [128, 512], F32, st)

                Bkv = [Buf("kv0"), Buf("kv1")]
                Bbt = [Buf("Hbuf0"), Buf("Hbuf1")]
                Bq = [Buf(f"q{i}") for i in range(NQB)]
                Bg = [Buf(f"g{i}") for i in range(4)]
                BpT = [Buf(f"pT{i}") for i in range(NPT)]
                BpsS = [Buf("psS0"), Buf("psS1")]
                BaccO, BaccZ = Buf("accO"), Buf("accZ")
                Brzp, Brzf = [Buf("rzp0"), Buf("rzp1")], Buf("rzf")
                Bob, Btb = [Buf("ob0"), Buf("ob1")], [Buf("tb0"), Buf("tb1")]
                Bsq = [Buf("sq0"), Buf("sq1")]
                Brsd = [Buf("rsd0"), Buf("rsd1")]
                Bmixo = [Buf("mixo0"), Buf("mixo1")]

                items = []
                for i in range(NP):
                    for h in range(4):
                        items.append(("p", i, "A", h))
                    for n in range(2):
                        items.append(("p", i, "B", n))
                for h in range(4):
                    items.append(("s", 0, "A", h))
                for n in range(2):
                    items.append(("s", 0, "B", n))

                def load_item(ii):
                    job, si, kind, idx = items[ii]
                    slot = ii % 2
                    kunit = idx if kind == "A" else 4 + idx
                    vcol = idx * 128 if kind == "A" else 512 + idx * 64
                    vw = 128 if kind == "A" else 64
                    if job == "p":
                        ksrc = dap(kTp_d, kunit * 128 * TP + si * SP, [[TP, 128], [1, SP]])
                        R.dma("sp", lambda e: e.dma_start(out=kbuf[slot][:, 0:SP], in_=ksrc), f"kv{slot}",
                              writes=[Bkv[slot]])
                        for g0 in range(0, KP, 16):
                            gn = min(16, KP - g0)
                            vsrc = dap(vp_d, (si * SP + g0 * 128) * 640 + vcol, [[640, 128], [128 * 640, gn], [1, vw]])
                            R.dma("sp", lambda e, g0=g0, gn=gn, vsrc=vsrc: e.dma_start(
                                out=vbuf[slot][:, g0:g0 + gn, 0:vw], in_=vsrc), f"kv{slot}", writes=[Bkv[slot]])
                    else:
                        kview = kbuf[slot][:, 0:SS].rearrange("p (j r t) -> p j r t", r=4, t=512)
                        for rr in range(4):
                            ksrc = dap(kTg_d[par][kunit], rr * 128 * TS, [[TS, 128], [512, NJ], [1, 512]])
                            R.dma("sp", lambda e, rr=rr, ksrc=ksrc: e.dma_start(
                                out=kview[:, :, rr, :], in_=ksrc), f"kv{slot}", writes=[Bkv[slot]])
                        for j in range(NJ):
                            vsrc = dap(vg_d[par][j], vcol, [[640, 128], [128 * 640, 16], [1, vw]])
                            R.dma("sp", lambda e, j=j, vsrc=vsrc: e.dma_start(
                                out=vbuf[slot][:, j * 16:(j + 1) * 16, 0:vw], in_=vsrc), f"kv{slot}", writes=[Bkv[slot]])

                a_items = [ii for ii, it in enumerate(items) if it[2] == "A"]
                hslot = {ii: k % 2 for k, ii in enumerate(a_items)}

                def load_btiles(ii):
                    job, si, kind, idx = items[ii]
                    if kind != "A":
                        return
                    hs = hslot[ii]
                    if job == "p":
                        src = dap(M_d, idx * LM, [[1, 128], [1, 1152]])
                        R.dma("sp", lambda e: e.dma_start(out=Hbuf[hs][:, 0:1152], in_=src), f"Hbuf{hs}",
                              writes=[Bbt[hs]])
                    else:
                        src = dap(M_d, idx * LM + LP_M, [[1, 128], [1, 2688]])
                        R.dma("sp", lambda e: e.dma_start(out=Hbuf[hs][:, :], in_=src), f"Hbuf{hs}",
                              writes=[Bbt[hs]])

                groups = []
                for ii, (job, si, kind, idx) in enumerate(items):
                    qunits = [idx] if kind == "A" else [4 + 2 * idx, 4 + 2 * idx + 1]
                    nq = QP if job == "p" else NJ
                    for qu in qunits:
                        for qc in range(nq):
                            groups.append((ii, qu, qc))

                def tok0(ii, qc):
                    job, si, kind, idx = items[ii]
                    return TS + si * SP + qc * 512 if job == "p" else qc * 512

                def load_qg(gi):
                    ii, qu, qc = groups[gi]
                    t0 = tok0(ii, qc)
                    qs = gi % NQB
                    gs = gi % 4
                    qsrc = dap(qT_d, qu * 128 * T + t0, [[T, 128], [1, 512]])
                    gsrc = dap(gT_d, qu * 128 * T + t0, [[T, 128], [1, 512]])
                    R.dma("sp", lambda e: e.dma_start(out=qbuf[qs][:], in_=qsrc), f"q{qs}", writes=[Bq[qs]])
                    R.dma("sp", lambda e: e.dma_start(out=gbuf[gs][:], in_=gsrc), f"g{gs}", writes=[Bg[gs]])

                state = {"sidx": 0, "pidx": 0, "eidx": 0, "zidx": 0, "zprev": 0}
                pending = []

                def emit_S(gi, kc):
                    ii, qu, qc = groups[gi]
                    job, si, kind, idx = items[ii]
                    slot = ii % 2
                    qs = gi % NQB
                    ss = state["sidx"] % 2
                    state["sidx"] += 1
                    zone = None
                    if kind == "A":
                        if job == "p":
                            w = kc - 4 * qc
                            lo, hi = -1, 4
                        else:
                            w = kc - 16 * qc
                            lo, hi = -1, 16
                        if w < lo:
                            zone = ("neg", None)
                        elif w > hi:
                            zone = ("pos", None)
                        else:
                            zone = ("near", w + 1)
                    near = zone is not None and zone[0] == "near"
                    kcols = slice(kc * 128, (kc + 1) * 128)
                    hs = hslot.get(ii, 0)
                    rd = [Bkv[slot], Bq[qs]] + ([Bbt[hs], B_const] if near else [])
                    for m in range(2):
                        pr = slice(m * 64, (m + 1) * 64)
                        oc = slice(m * 512, (m + 1) * 512)
                        R.op("pe", lambda e, ss=ss, pr=pr, oc=oc, slot=slot, kcols=kcols, qs=qs, near=near: e.matmul(
                            psS[ss][:, oc], lhsT=kbuf[slot][pr, kcols], rhs=qbuf[qs][pr, :], start=True,
                            stop=(not near)), reads=rd, writes=[BpsS[ss]])
                    if near:
                        w = zone[1] - 1
                        off = (512 - 128 * w) if job == "p" else (2048 - 128 * w)
                        for m in range(2):
                            oc = slice(m * 512, (m + 1) * 512)
                            R.op("pe", lambda e, ss=ss, oc=oc, off=off, hs=hs: e.matmul(
                                psS[ss][:, oc], lhsT=Jm, rhs=Hbuf[hs][:, off:off + 512], start=False, stop=True),
                                reads=rd, writes=[BpsS[ss]])
                    return ss, zone

                def emit_exp(gi, ss, zone):
                    ii, qu, qc = groups[gi]
                    job, si, kind, idx = items[ii]
                    pi = state["pidx"] % NPT
                    state["pidx"] += 1
                    if zone is None or zone[0] == "near":
                        R.op("act", lambda e, ss=ss, pi=pi: e.activation(out=pT[pi][:], in_=psS[ss][:, :], func=AF.Exp),
                             reads=[BpsS[ss]], writes=[BpT[pi]])
                    else:
                        col = idx if zone[0] == "neg" else 4 + idx
                        R.op("act", lambda e, ss=ss, pi=pi, col=col: e.activation(
                            out=pT[pi][:], in_=psS[ss][:, :], func=AF.Exp, bias=far[:, col:col + 1]),
                            reads=[BpsS[ss], Bsmall_dummy], writes=[BpT[pi]])
                    return pi

                def emit_PV(gi, kc, pi, first, lastk):
                    ii, qu, qc = groups[gi]
                    job, si, kind, idx = items[ii]
                    slot = ii % 2
                    rd = [Bkv[slot], BpT[pi], B_const]
                    if kind == "A":
                        for (acc, c0) in ((psO0, 0), (psO1, 512)):
                            R.op("pe", lambda e, acc=acc, c0=c0, slot=slot, kc=kc, pi=pi: e.matmul(
                                acc[:, :], lhsT=vbuf[slot][:, kc, :],
                                rhs=pT[pi][:, c0:c0 + 512], start=first, stop=lastk),
                                reads=rd, writes=[BaccO])
                        if kc % 2 == 0:
                            state["zprev"] = pi
                        else:
                            zi = state["zidx"] % 2
                            state["zidx"] += 1
                            p0 = state["zprev"]
                            R.op("dve", lambda e, zi=zi, p0=p0, pi=pi: e.tensor_tensor(
                                out=zsum[zi][:], in0=pT[p0][:], in1=pT[pi][:], op=ALU.add),
                                reads=[BpT[p0], BpT[pi]], writes=[Bzsum[zi]])
                            for m in range(2):
                                R.op("pe", lambda e, m=m, zi=zi: e.matmul(
                                    psZ0[m * 64:(m + 1) * 64, :], lhsT=ones_b[:, 0:64],
                                    rhs=zsum[zi][:, m * 512:(m + 1) * 512], start=(kc == 1), stop=lastk,
                                    tile_position=(0, m * 64)),
                                    reads=[Bzsum[zi], B_const], writes=[BaccZ])
                    else:
                        for (acc, lhs, bb) in ((psO0, None, BaccO), (psZ0, "ones", BaccZ)):
                            for m in range(2):
                                R.op("pe", lambda e, acc=acc, lhs=lhs, m=m, slot=slot, kc=kc, pi=pi: e.matmul(
                                    acc[m * 64:(m + 1) * 64, :],
                                    lhsT=(ones_b[:, 0:64] if lhs == "ones" else vbuf[slot][:, kc, 0:64]),
                                    rhs=pT[pi][:, m * 512:(m + 1) * 512], start=first, stop=lastk,
                                    tile_position=(0, m * 64)),
                                    reads=rd, writes=[bb])

                def emit_epilogue(gi):
                    ii, qu, qc = groups[gi]
                    job, si, kind, idx = items[ii]
                    t0 = tok0(ii, qc)
                    gs = gi % 4
                    ei = state["eidx"] % 2
                    state["eidx"] += 1
                    mdst = dap(mixT_d, qu * 128 * T + t0, [[T, 128], [1, 512]])
                    R.op("act", lambda e, ei=ei: e.activation(out=rzp[ei][:], in_=psZ0[:, :], func=AF.Ln),
                         reads=[BaccZ], writes=[Brzp[ei]])
                    R.op("act", lambda e, ei=ei: e.activation(out=rzp[ei][:], in_=rzp[ei][:], func=AF.Exp, scale=-1.0),
                         reads=[Brzp[ei]], writes=[Brzp[ei]])
                    if kind == "A":
                        R.op("dve", lambda e, ei=ei: e.tensor_copy(out=ob[ei][:], in_=psO0[:, :]),
                             reads=[BaccO], writes=[Bob[ei]])
                        R.op("dve", lambda e, ei=ei: e.tensor_copy(out=tb[ei][:], in_=psO1[:, :]),
                             reads=[BaccO], writes=[Btb[ei]])
                        war = list(Brzf.r.values())
                        R.dma("pool", lambda e, ei=ei: e.dma_start(out=rz0f[0:64, :], in_=rzp[ei][0:64, :]), "rzf",
                              reads=[Brzp[ei]], extra=war)
                        R.dma("pool", lambda e, ei=ei: e.dma_start(out=rz0f[64:128, :], in_=rzp[ei][0:64, :]), "rzf",
                              reads=[Brzp[ei]])
                        R.dma("pool", lambda e, ei=ei: e.dma_start(out=rz1f[0:64, :], in_=rzp[ei][64:128, :]), "rzf",
                              reads=[Brzp[ei]])
                        R.dma("pool", lambda e, ei=ei: e.dma_start(out=rz1f[64:128, :], in_=rzp[ei][64:128, :]), "rzf",
                              reads=[Brzp[ei]], writes=[Brzf])
                        R.op("dve", lambda e, ei=ei: e.tensor_tensor(out=ob[ei][:], in0=ob[ei][:], in1=rz0f[:], op=ALU.mult),
                             reads=[Bob[ei], Brzf], writes=[Bob[ei]])
                        R.op("dve", lambda e, ei=ei: e.tensor_tensor(out=tb[ei][:], in0=tb[ei][:], in1=rz1f[:], op=ALU.mult),
                             reads=[Btb[ei], Brzf], writes=[Btb[ei]])
                        R.op("dve", lambda e, ei=ei: e.scalar_tensor_tensor(
                            out=ob[ei][:], in0=tb[ei][:], scalar=neglam[:, l:l + 1], in1=ob[ei][:], op0=ALU.mult, op1=ALU.add),
                            reads=[Btb[ei], Bob[ei], B_const], writes=[Bob[ei]])
                        R.op("dve", lambda e, ei=ei: e.tensor_tensor(out=sq[ei][:], in0=ob[ei][:], in1=ob[ei][:], op=ALU.mult),
                             reads=[Bob[ei]], writes=[Bsq[ei]])

                        def tail(ei=ei, gs=gs, mdst=mdst):
                            ss = state["sidx"] % 2
                            state["sidx"] += 1
                            R.op("pe", lambda e, ss=ss, ei=ei: e.matmul(psS[ss][:, 0:512], lhsT=ones_b, rhs=sq[ei][:],
                                                                       start=True, stop=True),
                                 reads=[Bsq[ei], B_const], writes=[BpsS[ss]])
                            R.op("act", lambda e, ss=ss, ei=ei: e.activation(
                                out=rsd[ei][:], in_=psS[ss][:, 0:512], func=AF.Ln, scale=1.0 / 128, bias=EPS),
                                reads=[BpsS[ss]], writes=[Brsd[ei]])
                            R.op("act", lambda e, ei=ei: e.activation(out=rsd[ei][:], in_=rsd[ei][:], func=AF.Exp, scale=-0.5),
                                 reads=[Brsd[ei]], writes=[Brsd[ei]])
                            R.op("dve", lambda e, ei=ei: e.scalar_tensor_tensor(
                                out=ob[ei][:], in0=ob[ei][:], scalar=gsub[:, l:l + 1], in1=rsd[ei][:],
                                op0=ALU.mult, op1=ALU.mult), reads=[Bob[ei], Brsd[ei], B_const], writes=[Bob[ei]])
                            R.op("dve", lambda e, ei=ei, gs=gs: e.tensor_tensor(
                                out=mixo[ei][:], in0=ob[ei][:], in1=gbuf[gs][:], op=ALU.mult),
                                reads=[Bob[ei], Bg[gs]], writes=[Bmixo[ei]])
                            R.dma("pool", lambda e, ei=ei, mdst=mdst: e.dma_start(out=mdst, in_=mixo[ei][:]),
                                  f"mixo{ei}", reads=[Bmixo[ei]])
                        pending.append([7, tail])
                    else:
                        R.op("dve", lambda e, ei=ei: e.tensor_copy(out=ob[ei][:], in_=psO0[:, :]),
                             reads=[BaccO], writes=[Bob[ei]])
                        R.op("dve", lambda e, ei=ei: e.tensor_tensor(out=ob[ei][:], in0=ob[ei][:], in1=rzp[ei][:], op=ALU.mult),
                             reads=[Bob[ei], Brzp[ei]], writes=[Bob[ei]])
                        R.op("dve", lambda e, ei=ei, gs=gs: e.tensor_tensor(
                            out=mixo[ei][:], in0=ob[ei][:], in1=gbuf[gs][:], op=ALU.mult),
                            reads=[Bob[ei], Bg[gs]], writes=[Bmixo[ei]])
                        R.dma("pool", lambda e, ei=ei, mdst=mdst: e.dma_start(out=mdst, in_=mixo[ei][:]),
                              f"mixo{ei}", reads=[Bmixo[ei]])

                def tick_pending(force=False):
                    for p in list(pending):
                        p[0] -= 1
                        if force or p[0] <= 0:
                            pending.remove(p)
                            p[1]()

                Bsmall_dummy = B_const
                load_item(0)
                for _ii in a_items[:2]:
                    load_btiles(_ii)
                if len(items) > 1:
                    load_item(1)
                load_qg(0)
                if len(groups) > 1:
                    load_qg(1)
                flat = []
                for gi, (ii, qu, qc) in enumerate(groups):
                    nk = KP if items[ii][0] == "p" else KS
                    for kc in range(nk):
                        flat.append((gi, kc, kc == 0, kc == nk - 1))
                hist = []

                def flush_pv(next_gi):
                    pgi, pkc, ppi, pfirst, plast = hist.pop(0)
                    emit_PV(pgi, pkc, ppi, pfirst, plast)
                    if plast:
                        emit_epilogue(pgi)
                        pii = groups[pgi][0]
                        if next_gi is not None and groups[next_gi][0] != pii:
                            if pii + 2 < len(items):
                                load_item(pii + 2)
                    tick_pending()

                for fi, (gi, kc, first, lastk) in enumerate(flat):
                    ii = groups[gi][0]
                    if first:
                        if gi + 2 < len(groups):
                            load_qg(gi + 2)
                    ss, zone = emit_S(gi, kc)
                    pi = emit_exp(gi, ss, zone)
                    hist.append((gi, kc, pi, first, lastk))
                    if len(hist) > 2:
                        nxt = hist[0][0]
                        flush_pv(hist[1][0])
                    if lastk and (gi + 1 < len(groups)) and groups[gi + 1][0] != ii and ii in hslot:
                        k = a_items.index(ii)
                        if k + 2 < len(a_items):
                            load_btiles(a_items[k + 2])
                while hist:
                    flush_pv(hist[1][0] if len(hist) > 1 else None)
                tick_pending(force=True)
                R.barrier()

        def P4(l, par, last):
            with ExitStack() as st:
                Wo_b = sb("Wo_b", [128, 8, D], BF16, st)
                wos = [sb(f"wos{i}", [128, D], F32, st) for i in range(2)]
                mixin = [sb(f"mixin{i}", [128, 8, 512], BF16, st) for i in range(2)]
                xch = [sb(f"xch4_{i}", [128, 4, D], F32, st) for i in range(2)]
                gate = sb("gate_s", [128, NSEQ, D], F32, st)
                yb = [sb(f"yb{i}", [128, D], F32, st) for i in range(2)]
                xo = [sb(f"xo{i}", [128, D], F32, st) for i in range(2)]
                pmo = [ps(f"pmo{i}", [128, 512], F32, st) for i in range(4)]
                BWo = Buf("Wo")
                Bwos = [Buf("wos0"), Buf("wos1")]
                Bmixin = [Buf("mixin0"), Buf("mixin1")]
                Bxch = [Buf("xch40"), Buf("xch41")]
                Bgate = Buf("gate")
                Byb = [Buf("yb0"), Buf("yb1")]
                Bxo = [Buf("xo0"), Buf("xo1")]
                Bpmo = [Buf(f"pmo{i}") for i in range(4)]
                if last:
                    fg = sb("fg", [128, D], F32, st)
                    junk4 = sb("junk4", [128, D], F32, st)
                    ssq4 = sb("ssq4", [128, 2], F32, st)
                    rst4 = sb("rst4", [128, 2], F32, st)
                    yo = [sb(f"yo{i}", [128, D], F32, st) for i in range(2)]
                    Bfg, Bjunk4 = Buf("fg"), Buf("junk4")
                    Bssq4 = [Buf("ssq40"), Buf("ssq41")]
                    Brst4 = [Buf("rst40"), Buf("rst41")]
                    Byo = [Buf("yo0"), Buf("yo1")]
                    R.dma("sp", lambda e: e.dma_start(out=fg[:], in_=fg_d.ap()), "fg", writes=[Bfg])
                for mc in range(8):
                    slot = mc % 2
                    src = dap(w_out, l * D * D + mc * 128 * D, [[D, 128], [1, D]])
                    R.dma("sp", lambda e, slot=slot, src=src: e.dma_start(out=wos[slot][:], in_=src), f"wos{slot}",
                          writes=[Bwos[slot]])
                    R.op("dve" if mc % 2 == 0 else "pool", lambda e, slot=slot, mc=mc: e.tensor_copy(
                        out=Wo_b[:, mc, :], in_=wos[slot][:]), reads=[Bwos[slot]], writes=[BWo])
                gsrc = dap(gate_d, l * NSEQ * 128 * D, [[D, 128], [128 * D, NSEQ], [1, D]])
                R.dma("sp", lambda e: e.dma_start(out=gate[:], in_=gsrc), "gate", writes=[Bgate])
                mk = 0
                tk = 0
                for c in range(NCH):
                    s, is_s, loc = chunk_info(c)
                    cslot = c % 2
                    r0 = c * 512
                    msrc = dap(mixT_d, r0, [[T, 128], [128 * T, 8], [1, 512]])
                    R.dma("sp", lambda e, cslot=cslot, msrc=msrc: e.dma_start(out=mixin[cslot][:], in_=msrc),
                          f"mixin{cslot}", writes=[Bmixin[cslot]])
                    if l == 0:
                        xsrc = dap(x_s, loc * 512 * D, [[D, 128], [128 * D, 4], [1, D]]) if is_s else \
                            dap(x_p, loc * 512 * D, [[D, 128], [128 * D, 4], [1, D]])
                    else:
                        xsrc = dap(xbuf, r0 * D, [[D, 128], [128 * D, 4], [1, D]])
                    R.dma("sp", lambda e, cslot=cslot, xsrc=xsrc: e.dma_start(out=xch[cslot][:], in_=xsrc),
                          f"xch4_{cslot}", writes=[Bxch[cslot]])
                    for t in range(4):
                        ts_ = tk % 2
                        tk += 1
                        for half in range(2):
                            pmi = mk % 4
                            mk += 1
                            for mc in range(8):
                                R.op("pe", lambda e, pmi=pmi, mc=mc, t=t, half=half, cslot=cslot: e.matmul(
                                    pmo[pmi][:, :], lhsT=mixin[cslot][:, mc, t * 128:(t + 1) * 128],
                                    rhs=Wo_b[:, mc, half * 512:(half + 1) * 512], start=(mc == 0), stop=(mc == 7)),
                                    reads=[Bmixin[cslot], BWo], writes=[Bpmo[pmi]])
                            R.op("dve", lambda e, pmi=pmi, half=half, ts_=ts_, s=s: e.tensor_tensor(
                                out=yb[ts_][:, half * 512:(half + 1) * 512], in0=pmo[pmi][:, :],
                                in1=gate[:, s, half * 512:(half + 1) * 512], op=ALU.mult),
                                reads=[Bpmo[pmi], Bgate], writes=[Byb[ts_]])
                        R.op("pool", lambda e, ts_=ts_, cslot=cslot, t=t: e.tensor_tensor(
                            out=xo[ts_][:], in0=xch[cslot][:, t, :], in1=yb[ts_][:], op=ALU.add),
                            reads=[Bxch[cslot], Byb[ts_]], writes=[Bxo[ts_]])
                        if not last:
                            xdst = dap(xbuf, (r0 + t * 128) * D, [[D, 128], [1, D]])
                            R.dma("pool", lambda e, ts_=ts_, xdst=xdst: e.dma_start(out=xdst, in_=xo[ts_][:]),
                                  f"xo{ts_}", reads=[Bxo[ts_]])
                        else:
                            R.op("act", lambda e, ts_=ts_: e.activation(
                                out=junk4[:], in_=xo[ts_][:], func=AF.Square, accum_out=ssq4[:, ts_:ts_ + 1]),
                                reads=[Bxo[ts_]], writes=[Bjunk4, Bssq4[ts_]])
                            R.op("act", lambda e, ts_=ts_: e.activation(
                                out=rst4[:, ts_:ts_ + 1], in_=ssq4[:, ts_:ts_ + 1], func=AF.Ln, scale=1.0 / D, bias=EPS),
                                reads=[Bssq4[ts_]], writes=[Brst4[ts_]])
                            R.op("act", lambda e, ts_=ts_: e.activation(
                                out=rst4[:, ts_:ts_ + 1], in_=rst4[:, ts_:ts_ + 1], func=AF.Exp, scale=-0.5),
                                reads=[Brst4[ts_]], writes=[Brst4[ts_]])
                            R.op("dve", lambda e, ts_=ts_: e.scalar_tensor_tensor(
                                out=yo[ts_][:], in0=xo[ts_][:], scalar=rst4[:, ts_:ts_ + 1], in1=fg[:],
                                op0=ALU.mult, op1=ALU.mult), reads=[Bxo[ts_], Brst4[ts_], Bfg], writes=[Byo[ts_]])
                            if is_s:
                                ydst = dap(y_s, (loc * 512 + t * 128) * D, [[D, 128], [1, D]])
                            else:
                                ydst = dap(y_p, (loc * 512 + t * 128) * D, [[D, 128], [1, D]])
                            R.dma("pool", lambda e, ts_=ts_, ydst=ydst: e.dma_start(out=ydst, in_=yo[ts_][:]),
                                  f"yo{ts_}", reads=[Byo[ts_]])
                R.barrier()

        for l in range(L):
            par = l % 2
            last = (l == L - 1)
            if cfg.stop < 1:
                break
            P1(l, par, last)
            if cfg.stop < 3:
                break
            P3(l, par, last)
            if cfg.stop < 4:
                break
            P4(l, par, last)

        with nc.Block() as block:
            @block.tensor
            def _(e):
                R.replay("pe", e)

            @block.scalar
            def _(e):
                R.replay("act", e)

            @block.vector
            def _(e):
                R.replay("dve", e)

            @block.gpsimd
            def _(e):
                R.replay("pool", e)

            @block.sync
            def _(e):
                R.replay("sp", e)
    return nc


def t5_bucket_np(rel):
    rel = np.asarray(rel, dtype=np.int64)
    half, max_exact = 16, 8
    ret = np.where(rel > 0, half, 0)
    n = np.abs(rel)
    nf = np.maximum(n, 1).astype(np.float32)
    large = max_exact + (np.log(nf / np.float32(max_exact)) / np.float32(math.log(128 / max_exact))
                         * np.float32(half - max_exact)).astype(np.int32)
    large = np.minimum(large, half - 1)
    return ret + np.where(n < max_exact, n, large)


def rope_tables(pos):
    pos = np.asarray(pos)
    row = (pos // 64).astype(np.float32)
    col = (pos % 64).astype(np.float32)
    inv = (np.float32(10000.0) ** (-np.arange(0, 32, 2, dtype=np.float32) / np.float32(32))).astype(np.float32)
    ang_r = row[:, None] * inv[None, :]
    ang_c = col[:, None] * inv[None, :]
    ang = np.concatenate([ang_r, ang_r, ang_c, ang_c], axis=-1).astype(np.float32)
    return np.cos(ang).T.astype(np.float32), np.sin(ang).T.astype(np.float32)


def make_consts():
    c = np.zeros((128, 640), np.float32)
    c[:, 0:128] = np.eye(128)
    c[:, 128:256] = np.eye(128)[::-1]
    c[:, 256:384] = 1.0
    c[0:64, 384:448] = 1.0
    c[64:128, 448:512] = 1.0
    Rm = np.zeros((64, 64), np.float32)
    for d in range(64):
        q = d // 16
        if q % 2 == 0:
            Rm[d + 16, d] = -1.0
        else:
            Rm[d - 16, d] = 1.0
    c[0:64, 512:576] = Rm
    c[64:128, 576:640] = Rm
    return c


def make_core_inputs(cfg, core, inp):
    L, NP, SP, NJ = cfg.L, cfg.NP, cfg.SP, cfg.NJ
    NSEQ = 1 + NP
    TS, TP = NJ * 512, NP * SP
    T = TS + TP
    sseq, rank = core // 4, core % 4
    xs_full = inp["x_sample"][sseq]
    chunks = [xs_full[(4 * j + rank) * 512:(4 * j + rank + 1) * 512] for j in range(NJ)]
    x_s = np.ascontiguousarray(np.concatenate(chunks, 0))
    x_p = np.ascontiguousarray(inp["x_prompt"][core * NP:(core + 1) * NP].reshape(TP, D))
    cs_ = np.concatenate([inp["c_sample"][sseq:sseq + 1], inp["c_prompt"][core * NP:(core + 1) * NP]], 0)
    cT = np.ascontiguousarray(cs_.reshape(NSEQ, 8, 128).transpose(2, 1, 0).reshape(128, 8 * NSEQ))
    b_ada = inp["b_ada"]
    b_adaT = np.ascontiguousarray(b_ada[:, :2048].reshape(L, 16, 128).transpose(2, 0, 1).reshape(128, L * 16))
    b_gate = np.ascontiguousarray(b_ada[:, 2048:3072])
    ngT = np.ascontiguousarray(inp["norm_g"].reshape(L, 8, 128).transpose(2, 0, 1).reshape(128, L * 8))
    smallp = np.zeros((128, 3 * L + 8), np.float32)
    smallp[:, 0:L] = inp["subln_g"].T
    smallp[:, L:2 * L] = np.concatenate([inp["q_norm_g"].T, inp["q_norm_g"].T], 0)
    smallp[:, 2 * L:3 * L] = np.concatenate([inp["k_norm_g"].T, inp["k_norm_g"].T], 0)
    smallp[:, 3 * L:3 * L + 4] = inp["rel_table"][15][None, :]
    smallp[:, 3 * L + 4:3 * L + 8] = inp["rel_table"][31][None, :]
    lamrow = np.concatenate([np.concatenate([inp["lam_q1"][l], inp["lam_k1"][l], inp["lam_q2"][l], inp["lam_k2"][l]])
                             for l in range(L)])
    lamv = np.ascontiguousarray(np.broadcast_to(lamrow[None, :], (128, L * 256))).astype(np.float32)
    fgbc = np.ascontiguousarray(np.broadcast_to(inp["final_g"][None, :], (128, D))).astype(np.float32)
    oh = np.zeros((32, LM), np.float32)
    n = np.arange(LP_M)
    oh[t5_bucket_np(639 - n), n] = 1.0
    n = np.arange(LS_M)
    oh[t5_bucket_np(2175 - 512 * rank - n), LP_M + n] = 1.0
    pos_s = np.concatenate([np.arange((4 * j + rank) * 512, (4 * j + rank + 1) * 512) for j in range(NJ)])
    pos = np.concatenate([pos_s] + [np.arange(SP)] * NP)
    cos, sin = rope_tables(pos)
    cstab = np.zeros((128, 2, T), np.float32)
    cstab[0:64, 0], cstab[64:128, 0] = cos, cos
    cstab[0:64, 1], cstab[64:128, 1] = sin, sin
    return {
        "x_s": x_s, "x_p": x_p, "cT": cT, "w_ada": inp["w_ada"], "b_adaT": b_adaT, "b_gate": b_gate,
        "w_in": inp["w_in"], "w_out": inp["w_out"], "ngT": ngT, "smallp": smallp, "lamv": lamv, "fgbc": fgbc,
        "table": np.ascontiguousarray(inp["rel_table"]), "onehot": oh, "cstab": cstab, "consts": make_consts(),
    }


def run_cfg(cfg, inp):
    nc = build_program(cfg)
    in_maps = [make_core_inputs(cfg, core, inp) for core in range(8)]
    res = run_bass_kernel_spmd(nc, in_maps, core_ids=list(range(8)))
    if cfg.debug:
        return res
    L, NP, SP, NJ = cfg.L, cfg.NP, cfg.SP, cfg.NJ
    SS = 4 * NJ * 512
    y_p = np.zeros((8 * NP, SP, D), np.float32)
    y_s = np.zeros((2, SS, D), np.float32)
    for core in range(8):
        r = res.results[core]
        y_p[core * NP:(core + 1) * NP] = np.asarray(r["y_p"]).reshape(NP, SP, D)
        sseq, rank = core // 4, core % 4
        ys = np.asarray(r["y_s"])
        for j in range(NJ):
            y_s[sseq, (4 * j + rank) * 512:(4 * j + rank + 1) * 512] = ys[j * 512:(j + 1) * 512]
    return y_p, y_s


def kernel(x_prompt, x_sample, c_prompt, c_sample, rel_table, norm_g, w_ada, b_ada, w_in,
           lam_q1, lam_k1, lam_q2, lam_k2, subln_g, q_norm_g, k_norm_g, w_out, final_g):
    f = lambda a: np.ascontiguousarray(np.asarray(a, dtype=np.float32))
    inp = dict(x_prompt=f(x_prompt), x_sample=f(x_sample), c_prompt=f(c_prompt), c_sample=f(c_sample),
               rel_table=f(rel_table), norm_g=f(norm_g), w_ada=f(w_ada), b_ada=f(b_ada), w_in=f(w_in),
               lam_q1=f(lam_q1), lam_k1=f(lam_k1), lam_q2=f(lam_q2), lam_k2=f(lam_k2), subln_g=f(subln_g),
               q_norm_g=f(q_norm_g), k_norm_g=f(k_norm_g), w_out=f(w_out), final_g=f(final_g))
    cfg = Cfg(L=4, NP=4, SP=2048, NJ=8)
    y_p, y_s = run_cfg(cfg, inp)
    return (y_p, y_s)
```

```python
import math
from contextlib import ExitStack

import numpy as np
import ml_dtypes

import concourse.bass as bass
import concourse.mybir as mybir
from concourse.bass_utils import run_bass_kernel_spmd

F32 = mybir.dt.float32
BF16 = mybir.dt.bfloat16
AF = mybir.ActivationFunctionType
ALU = mybir.AluOpType
AX = mybir.AxisListType

D = 1024
EPS = 1e-6
NQ7 = 22 * 128
WCOLS = NQ7 + 640
LP_M = 1280
LS_M = 2816
LM = LP_M + LS_M


class Cfg:
    def __init__(self, L=4, NP=4, SP=2048, NJ=8, stop=9, noag=False, debug=False):
        self.debug = debug
        self.L, self.NP, self.SP, self.NJ = L, NP, SP, NJ
        self.stop = stop
        self.noag = noag


class Buf:
    __slots__ = ("name", "w", "r")

    def __init__(self, name):
        self.name = name
        self.w = None
        self.r = {}


class Rec:
    ENG = ("pe", "act", "dve", "pool", "sp")

    def __init__(self, nc):
        self.nc = nc
        self.stream = {e: [] for e in self.ENG}
        self.sem = {e: nc.alloc_semaphore("s_" + e) for e in self.ENG}
        self.cnt = {e: 0 for e in self.ENG}
        self.waited = {e: {} for e in self.ENG}
        self.dsem = {}

    def _collect(self, eng, reads, writes, extra):
        evs = []
        for b in reads:
            if b.w is not None:
                evs.append((b.w, "raw"))
        for b in writes:
            for ev in b.r.values():
                evs.append((ev, "war"))
            if b.w is not None:
                evs.append((b.w, "waw"))
        for ev in extra:
            if ev is not None:
                evs.append((ev, "raw"))
        out = []
        for (ev, kind) in evs:
            h, val, key = ev
            if key == eng:
                if eng == "pe" or kind != "raw":
                    continue
            if self.waited[eng].get(key, 0) >= val:
                continue
            self.waited[eng][key] = val
            out.append((h, val))
        return out

    def op(self, eng, fn, reads=(), writes=(), extra=()):
        wl = self._collect(eng, reads, writes, extra)
        self.cnt[eng] += 1
        ev = (self.sem[eng], self.cnt[eng], eng)
        self.stream[eng].append((wl, fn, (self.sem[eng], 1)))
        for b in reads:
            b.r[eng] = ev
        for b in writes:
            b.w = ev
            b.r = {}
        return ev

    def dma(self, q, fn, semname, reads=(), writes=(), extra=(), inc=16):
        wl = self._collect(q, reads, writes, extra)
        key = "d_" + semname
        if semname not in self.dsem:
            self.dsem[semname] = [self.nc.alloc_semaphore(key), 0]
        d = self.dsem[semname]
        d[1] += inc
        ev = (d[0], d[1], key)
        self.stream[q].append((wl, fn, (d[0], inc)))
        for b in reads:
            b.r[key] = ev
        for b in writes:
            b.w = ev
            b.r = {}
        return ev

    def barrier(self):
        targets = []
        for e in self.ENG:
            if self.cnt[e] > 0:
                targets.append((self.sem[e], self.cnt[e], e))
        for name, d in self.dsem.items():
            if d[1] > 0:
                targets.append((d[0], d[1], "d_" + name))
        for e in self.ENG:
            wl = []
            for (h, val, key) in targets:
                if key == e:
                    continue
                if self.waited[e].get(key, 0) >= val:
                    continue
                self.waited[e][key] = val
                wl.append((h, val))
            if wl:
                self.stream[e].append((wl, None, None))

    def replay(self, eng, e):
        for (wl, fn, inc) in self.stream[eng]:
            for (h, val) in wl:
                e.wait_ge(h, val)
            if fn is not None:
                ins = fn(e)
                if inc is not None:
                    ins.then_inc(inc[0], inc[1])


def dap(handle, offset, dims):
    return bass.AP(handle, offset, [list(d) for d in dims])


def build_program(cfg):
    L, NP, SP, NJ = cfg.L, cfg.NP, cfg.SP, cfg.NJ
    NSEQ = 1 + NP
    TS = NJ * 512
    TP = NP * SP
    T = TS + TP
    NCH = T // 512
    SS = 4 * TS
    QP = SP // 512
    KP = SP // 128
    KS = SS // 128
    KMAX = max(KP, KS)
    SMAX = max(SP, SS)

    nc = bass.Bass("TRN2", target_bir_lowering=False)

    def din(name, shape, dt=F32):
        return nc.dram_tensor(name, list(shape), dt, kind="ExternalInput")

    def dscr(name, shape, dt):
        if cfg.debug and name in ("qT_d", "gT_d", "kTp_d", "vp_d", "mixT_d", "M_d", "gate_d", "xbuf"):
            return nc.dram_tensor(name, list(shape), dt, kind="ExternalOutput")
        return nc.dram_tensor(name, list(shape), dt)

    x_s = din("x_s", [TS, D])
    x_p = din("x_p", [TP, D])
    cT_d = din("cT", [128, 8 * NSEQ])
    w_ada = din("w_ada", [L, D, 3 * D])
    b_adaT = din("b_adaT", [128, L * 16])
    b_gate = din("b_gate", [L, D])
    w_in = din("w_in", [L, D, 3328])
    w_out = din("w_out", [L, D, D])
    ngT_d = din("ngT", [128, L * 8])
    smallp_d = din("smallp", [128, 3 * L + 8])
    lamv_d = din("lamv", [128, L * 256])
    fg_d = din("fgbc", [128, D])
    table_d = din("table", [32, 4])
    oh_d = din("onehot", [32, LM])
    cs_d = din("cstab", [128, 2, T])
    consts_d = din("consts", [128, 640])
    y_s = nc.dram_tensor("y_s", [TS, D], F32, kind="ExternalOutput")
    y_p = nc.dram_tensor("y_p", [TP, D], F32, kind="ExternalOutput")

    xbuf = dscr("xbuf", [T, D], F32)
    qT_d = dscr("qT_d", [8, 128, T], BF16)
    gT_d = dscr("gT_d", [8, 128, T], BF16)
    kTp_d = dscr("kTp_d", [6, 128, TP], BF16)
    vp_d = dscr("vp_d", [TP, 640], BF16)
    kTc_d = [[dscr(f"kTc_d{i}_{k}", [128, TS], BF16) for k in range(6)] for i in range(2)]
    kTg_d = [[dscr(f"kTg_d{i}_{k}", [4 * 128, TS], BF16) for k in range(6)] for i in range(2)]
    vc_d = [[dscr(f"vc_d{i}_{j}", [512, 640], BF16) for j in range(NJ)] for i in range(2)]
    vg_d = [[dscr(f"vg_d{i}_{j}", [4 * 512, 640], BF16) for j in range(NJ)] for i in range(2)]
    mixT_d = dscr("mixT_d", [D, T], BF16)
    M_d = dscr("M_d", [4, LM], BF16)
    gate_d = dscr("gate_d", [L * NSEQ, 128, D], F32)

    R = Rec(nc)

    def chunk_info(c):
        if c < NJ:
            return 0, True, c
        pc = c - NJ
        return 1 + pc // QP, False, pc

    with ExitStack() as top:
        uid = [0]

        def sb(name, shape, dt, st=top):
            uid[0] += 1
            return st.enter_context(nc.sbuf_tensor(f"{name}_u{uid[0]}", list(shape), dt))

        def ps(name, shape, dt, st):
            uid[0] += 1
            return st.enter_context(nc.psum_tensor(f"{name}_u{uid[0]}", list(shape), dt))

        consts_b = sb("consts_b", [128, 640], BF16)
        ident = consts_b[:, 0:128]
        Jm = consts_b[:, 128:256]
        ones_b = consts_b[:, 256:384]
        bones = consts_b[:, 384:512]
        rrot = consts_b[:, 512:640]
        ngT = sb("ngT_s", [128, L * 8], F32)
        smallp = sb("smallp_s", [128, 3 * L + 8], F32)
        far = smallp[:, 3 * L:3 * L + 8]
        neglam = sb("neglam", [128, L], F32)
        gsub = sb("gsub", [128, L], F32)
        qg2 = sb("qg2", [128, L], F32)
        AT = sb("AT", [128, L * NSEQ * 8], F32)
        BT = sb("BT", [128, L * NSEQ * 8], F32)
        B_const = Buf("consts")

        with ExitStack() as st:
            consts_f = sb("consts_f", [128, 640], F32, st)
            lamv = sb("lamv_s", [128, L * 256], F32, st)
            lamt = sb("lamt", [128, 64], F32, st)
            lams = sb("lams", [128, 2 * L], F32, st)
            lame = sb("lame", [128, 2 * L], F32, st)
            cT = sb("cT_s", [128, 8 * NSEQ], F32, st)
            cact = sb("cact", [128, 8 * NSEQ], F32, st)
            cact_b = sb("cact_b", [128, 8 * NSEQ], BF16, st)
            ones_f = sb("ones_f", [128, 128], F32, st)
            crep = sb("crep", [128, NSEQ * 8, 128], BF16, st)
            b_adaT_s = sb("b_adaT_s", [128, L * 16], F32, st)
            modT = sb("modT", [128, L * 16 * NSEQ], F32, st)
            bg = sb("bg", [128, D], F32, st)
            wst = [sb(f"wast{i}", [128, 8, 512], F32, st) for i in range(2)]
            wab = [sb(f"wab{i}", [128, 8, 512], BF16, st) for i in range(2)]
            gsb = [sb(f"gsb{i}", [128, 512], F32, st) for i in range(2)]
            tab_f = sb("tab_f", [32, 4], F32, st)
            tab_b = sb("tab_b", [32, 4], BF16, st)
            oh_f = sb("oh_f", [32, LM], F32, st)
            oh_b = sb("oh_b", [32, LM], BF16, st)
            M_s = sb("M_s", [4, LM], BF16, st)
            tmp8 = sb("tmp8", [128, 8], F32, st)
            pmod = ps("pmod", [128, 512], F32, st)
            pgate = [ps(f"pgate{i}", [128, 512], F32, st) for i in range(2)]
            pM = [ps(f"pM{i}", [4, 512], F32, st) for i in range(2)]

            Bc_f, Blamv, BcT, Bbada, Btab, Boh = (Buf(n) for n in ("cf", "lamv", "cT", "bada", "tab", "oh"))
            Bsmall, BngT = Buf("small"), Buf("ngT")
            R.dma("sp", lambda e: e.dma_start(out=consts_f[:], in_=consts_d.ap()), "cf", writes=[Bc_f])
            R.dma("sp", lambda e: e.dma_start(out=lamv[:], in_=lamv_d.ap()), "lamv", writes=[Blamv])
            R.dma("sp", lambda e: e.dma_start(out=cT[:], in_=cT_d.ap()), "cT", writes=[BcT])
            R.dma("sp", lambda e: e.dma_start(out=b_adaT_s[:], in_=b_adaT.ap()), "bada", writes=[Bbada])
            R.dma("sp", lambda e: e.dma_start(out=tab_f[:], in_=table_d.ap()), "tab", writes=[Btab])
            R.dma("sp", lambda e: e.dma_start(out=oh_f[:], in_=oh_d.ap()), "oh", writes=[Boh])
            R.dma("sp", lambda e: e.dma_start(out=smallp[:], in_=smallp_d.ap()), "small", writes=[Bsmall])
            R.dma("sp", lambda e: e.dma_start(out=ngT[:], in_=ngT_d.ap()), "ngT", writes=[BngT])

            R.op("dve", lambda e: e.tensor_copy(out=consts_b[:], in_=consts_f[:]), reads=[Bc_f], writes=[B_const])
            Bones_f = Buf("ones_f")
            R.op("dve", lambda e: e.memset(ones_f[:], 1.0), writes=[Bones_f])
            Blamt, Blams, Blame, Bneglam = Buf("lamt"), Buf("lams"), Buf("lame"), Buf("neglam")
            for l in range(L):
                for k in range(2):
                    a0 = l * 256 + k * 128
                    R.op("dve", lambda e, a0=a0: e.tensor_tensor(out=lamt[:], in0=lamv[:, a0:a0 + 64],
                                                                 in1=lamv[:, a0 + 64:a0 + 128], op=ALU.mult),
                         reads=[Blamv], writes=[Blamt])
                    R.op("dve", lambda e, l=l, k=k: e.reduce_sum(out=lams[:, 2 * l + k:2 * l + k + 1], in_=lamt[:],
                                                                 axis=AX.X),
                         reads=[Blamt], writes=[Blams])
            R.op("act", lambda e: e.activation(out=lame[:], in_=lams[:], func=AF.Exp), reads=[Blams], writes=[Blame])
            for l in range(L):
                lam_init = 0.8 - 0.6 * math.exp(-0.3 * l)
                R.op("dve", lambda e, l=l: e.tensor_tensor(out=neglam[:, l:l + 1], in0=lame[:, 2 * l + 1:2 * l + 2],
                                                           in1=lame[:, 2 * l:2 * l + 1], op=ALU.subtract),
                     reads=[Blame], writes=[Bneglam])
                R.op("dve", lambda e, l=l, li=lam_init: e.tensor_scalar_add(out=neglam[:, l:l + 1],
                                                                           in0=neglam[:, l:l + 1], scalar1=-li),
                     reads=[Bneglam], writes=[Bneglam])
                R.op("dve", lambda e, l=l, li=lam_init: e.tensor_scalar_mul(out=gsub[:, l:l + 1],
                                                                           in0=smallp[:, l:l + 1], scalar1=1.0 - li),
                     reads=[Bsmall], writes=[B_const])
                R.op("dve", lambda e, l=l: e.tensor_scalar_mul(out=qg2[:, l:l + 1],
                                                               in0=smallp[:, L + l:L + l + 1], scalar1=0.125),
                     reads=[Bsmall], writes=[B_const])
            Bcact, Bcactb, Bcrep = Buf("cact"), Buf("cactb"), Buf("crep")
            R.op("act", lambda e: e.activation(out=cact[:], in_=cT[:], func=AF.Silu), reads=[BcT], writes=[Bcact])
            R.op("dve", lambda e: e.tensor_copy(out=cact_b[:], in_=cact[:]), reads=[Bcact], writes=[Bcactb])
            for s in range(NSEQ):
                for fc in range(8):
                    i = fc * NSEQ + s
                    R.op("dve", lambda e, i=i, s=s, fc=fc: e.tensor_scalar(
                        out=crep[:, s * 8 + fc, :], in0=ones_f[:], scalar1=cact[:, i:i + 1], scalar2=None,
                        op0=ALU.mult), reads=[Bcact, Bones_f], writes=[Bcrep])
            Btabb, Bohb, BMs = Buf("tabb"), Buf("ohb"), Buf("Ms")
            R.op("dve", lambda e: e.tensor_copy(out=tab_b[:], in_=tab_f[:]), reads=[Btab], writes=[Btabb])
            R.op("pool", lambda e: e.tensor_copy(out=oh_b[:], in_=oh_f[:]), reads=[Boh], writes=[Bohb])
            BpM = [Buf("pM0"), Buf("pM1")]
            for k in range(LM // 512):
                R.op("pe", lambda e, k=k: e.matmul(pM[k % 2][:, :], lhsT=tab_b[:, :], rhs=oh_b[:, k * 512:(k + 1) * 512],
                                                   start=True, stop=True),
                     reads=[Btabb, Bohb], writes=[BpM[k % 2]])
                R.op("dve", lambda e, k=k: e.tensor_copy(out=M_s[:, k * 512:(k + 1) * 512], in_=pM[k % 2][:, :]),
                     reads=[BpM[k % 2]], writes=[BMs])
            BMd = Buf("Md")
            R.dma("sp", lambda e: e.dma_start(out=M_d.ap(), in_=M_s[:]), "Mst", reads=[BMs], writes=[BMd])

            Bwst = [Buf("wst0"), Buf("wst1")]
            Bwab = [Buf("wab0"), Buf("wab1")]
            Bpmod, BmodT, Bbg = Buf("pmod"), Buf("modT"), Buf("bg")
            Bpg = [Buf("pg0"), Buf("pg1")]
            Bgsb = [Buf("gsb0"), Buf("gsb1")]
            k = 0
            gcount = 0
            for l in range(L):
                R.dma("sp", lambda e, l=l: e.dma_start(out=bg[:], in_=b_gate[l:l + 1, :].partition_broadcast(128)),
                      "bg", writes=[Bbg])
                for nb in range(6):
                    slot = k % 2
                    k += 1
                    src = dap(w_ada, l * D * 3 * D + nb * 512, [[3 * D, 128], [128 * 3 * D, 8], [1, 512]])
                    R.dma("sp", lambda e, slot=slot, src=src: e.dma_start(out=wst[slot][:], in_=src),
                          f"wast{slot}", writes=[Bwst[slot]])
                    R.op("pool", lambda e, slot=slot: e.tensor_copy(out=wab[slot][:], in_=wst[slot][:]),
                         reads=[Bwst[slot]], writes=[Bwab[slot]])
                    if nb < 4:
                        for cb in range(4):
                            blk = nb * 4 + cb
                            for fc in range(8):
                                R.op("pe", lambda e, slot=slot, cb=cb, fc=fc: e.matmul(
                                    pmod[:, 0:NSEQ], lhsT=wab[slot][:, fc, cb * 128:(cb + 1) * 128],
                                    rhs=cact_b[:, fc * NSEQ:(fc + 1) * NSEQ], start=(fc == 0), stop=(fc == 7)),
                                    reads=[Bwab[slot], Bcactb], writes=[Bpmod])
                            o0 = (l * 16 + blk) * NSEQ
                            R.op("dve", lambda e, o0=o0, l=l, blk=blk: e.tensor_scalar(
                                out=modT[:, o0:o0 + NSEQ], in0=pmod[:, 0:NSEQ],
                                scalar1=b_adaT_s[:, l * 16 + blk:l * 16 + blk + 1], scalar2=None, op0=ALU.add),
                                reads=[Bpmod, Bbada], writes=[BmodT])
                    else:
                        half = nb - 4
                        for s in range(NSEQ):
                            g = gcount % 2
                            gcount += 1
                            for fc in range(8):
                                R.op("pe", lambda e, g=g, s=s, fc=fc, slot=slot: e.matmul(
                                    pgate[g][:, :], lhsT=crep[:, s * 8 + fc, :], rhs=wab[slot][:, fc, :],
                                    start=(fc == 0), stop=(fc == 7)),
                                    reads=[Bwab[slot], Bcrep], writes=[Bpg[g]])
                            R.op("dve", lambda e, g=g, half=half: e.tensor_tensor(
                                out=gsb[g][:], in0=pgate[g][:, :], in1=bg[:, half * 512:(half + 1) * 512], op=ALU.add),
                                reads=[Bpg[g], Bbg], writes=[Bgsb[g]])
                            dst = dap(gate_d, (l * NSEQ + s) * 128 * D + half * 512, [[D, 128], [1, 512]])
                            R.dma("sp", lambda e, g=g, dst=dst: e.dma_start(out=dst, in_=gsb[g][:]),
                                  f"gsb{g}", reads=[Bgsb[g]])
                Btmp8 = Buf("tmp8")
                for s in range(NSEQ):
                    base = l * 16 * NSEQ
                    o = (l * NSEQ + s) * 8
                    sc0 = base + 8 * NSEQ + s
                    sh0 = base + s
                    R.op("dve", lambda e, sc0=sc0: e.tensor_scalar_add(
                        out=tmp8[:], in0=modT[:, sc0:sc0 + 7 * NSEQ + 1:NSEQ], scalar1=1.0),
                        reads=[BmodT], writes=[Btmp8])
                    R.op("dve", lambda e, o=o, l=l: e.tensor_tensor(
                        out=AT[:, o:o + 8], in0=tmp8[:], in1=ngT[:, l * 8:(l + 1) * 8], op=ALU.mult),
                        reads=[Btmp8, BngT], writes=[B_const])
                    R.op("dve", lambda e, o=o, sh0=sh0: e.tensor_copy(
                        out=BT[:, o:o + 8], in_=modT[:, sh0:sh0 + 7 * NSEQ + 1:NSEQ]),
                        reads=[BmodT], writes=[B_const])
            R.barrier()

        def P1(l, par, last):
            with ExitStack() as st:
                W_b = sb("W_b", [128, 8, WCOLS], BF16, st)
                wstage = [sb("wstage0", [128, 3328], F32, st)]
                xch = [sb(f"xch{i}", [128, 4, D], F32, st) for i in range(2)]
                junk = sb("junk", [128, D], F32, st)
                ssq = sb("ssq", [128, 4], F32, st)
                rstd = sb("rstd", [128, 4], F32, st)
                xs = [sb(f"xs{i}", [128, 4, D], BF16, st) for i in range(2)]
                hT = [sb(f"hT{i}", [128, 8, 512], BF16, st) for i in range(2)]
                NFM = 6
                fmo = [sb(f"fmo{i}", [128, 512], BF16, st) for i in range(NFM)]
                vout = [sb(f"vout{i}", [128, 640], BF16, st) for i in range(2)]
                cs = [sb(f"cs{i}", [128, 2, 512], F32, st) for i in range(2)]
                sqb = [sb(f"sqb{i}", [128, 512], BF16, st) for i in range(3)]
                rs = [sb(f"rs{i}", [128, 512], F32, st) for i in range(3)]
                qnf = [sb(f"qnf{i}", [128, 512], F32, st) for i in range(3)]
                qnb = [sb(f"qnb{i}", [128, 512], BF16, st) for i in range(3)]
                t2 = [sb(f"t2{i}", [128, 512], F32, st) for i in range(3)]
                ptr = [ps(f"ptr{i}", [128, 1024], BF16, st) for i in range(2)]
                pmm = [ps(f"pmm{i}", [128, 512], F32, st) for i in range(4)]
                paux = [ps(f"paux{i}", [128, 512], F32, st) for i in range(2)]

                BW = Buf("W_b")
                Bwstage = [Buf("wstage0")]
                Bxch = [Buf("xch0"), Buf("xch1")]
                Bjunk, Bssq, Brstd = Buf("junk"), Buf("ssq"), Buf("rstd")
                Bxs = [Buf("xs0"), Buf("xs1")]
                BhT = [Buf("hT0"), Buf("hT1")]
                Bfmo = [Buf(f"fmo{i}") for i in range(NFM)]
                Bvout = [Buf("vout0"), Buf("vout1")]
                Bcs = [Buf("cs0"), Buf("cs1")]
                Bsqb = [Buf(f"sqb{i}") for i in range(3)]
                Brs = [Buf(f"rs{i}") for i in range(3)]
                Bqnf = [Buf(f"qnf{i}") for i in range(3)]
                Bqnb = [Buf(f"qnb{i}") for i in range(3)]
                Bt2 = [Buf(f"t2{i}") for i in range(3)]
                Bptr = [Buf("ptr0"), Buf("ptr1")]
                Bpmm = [Buf(f"pmm{i}") for i in range(4)]
                Bpaux = [Buf("paux0"), Buf("paux1")]
                BkTc, Bvc = Buf("kTc"), Buf("vc")

                casts = [
                    (0, 0, 512), (512, 512, 512), (1024, 1536, 512), (1536, 2048, 512),
                    (2048, 2560, 64), (2112, 2560, 64), (2176, 2624, 64), (2240, 2624, 64),
                    (2304, 2816, 512), (2816, 1024, 512), (3328, 2688, 128)]
                for fc in range(8):
                    slot = 0
                    src = dap(w_in, l * D * 3328 + fc * 128 * 3328, [[3328, 128], [1, 3328]])
                    R.dma("sp", lambda e, slot=slot, src=src: e.dma_start(out=wstage[slot][:], in_=src),
                          f"wstage{slot}", writes=[Bwstage[slot]])
                    for ci, (dc, sc, w) in enumerate(casts):
                        eng = "dve" if ci % 2 == 0 else "pool"
                        R.op(eng, lambda e, slot=slot, fc=fc, dc=dc, sc=sc, w=w: e.tensor_copy(
                            out=W_b[:, fc, dc:dc + w], in_=wstage[slot][:, sc:sc + w]),
                            reads=[Bwstage[slot]], writes=[BW])

                cnt = {"mm": 0, "fm": 0, "aux": 0}

                def emit_xload(c):
                    s_, is_s_, loc_ = chunk_info(c)
                    cslot_ = c % 2
                    r0_ = c * 512
                    if l == 0:
                        xsrc = dap(x_s, loc_ * 512 * D, [[D, 128], [128 * D, 4], [1, D]]) if is_s_ else \
                            dap(x_p, loc_ * 512 * D, [[D, 128], [128 * D, 4], [1, D]])
                    else:
                        xsrc = dap(xbuf, r0_ * D, [[D, 128], [128 * D, 4], [1, D]])
                    R.dma("sp", lambda e: e.dma_start(out=xch[cslot_][:], in_=xsrc),
                          f"xch{cslot_}", writes=[Bxch[cslot_]])

                def emit_csload(c):
                    cslot_ = c % 2
                    cssrc = dap(cs_d, c * 512, [[2 * T, 128], [T, 2], [1, 512]])
                    R.dma("sp", lambda e: e.dma_start(out=cs[cslot_][:], in_=cssrc),
                          f"cs{cslot_}", writes=[Bcs[cslot_]])

                def fe_act(c):
                    cslot = c % 2
                    for t in range(4):
                        R.op("act", lambda e, t=t: e.activation(
                            out=junk[:], in_=xch[cslot][:, t, :], func=AF.Square, accum_out=ssq[:, t:t + 1]),
                            reads=[Bxch[cslot]], writes=[Bjunk, Bssq])
                    R.op("act", lambda e: e.activation(out=rstd[:], in_=ssq[:], func=AF.Ln, scale=1.0 / D, bias=EPS),
                         reads=[Bssq], writes=[Brstd])
                    R.op("act", lambda e: e.activation(out=rstd[:], in_=rstd[:], func=AF.Exp, scale=-0.5),
                         reads=[Brstd], writes=[Brstd])
                    for t in range(4):
                        R.op("act", lambda e, t=t: e.activation(
                            out=xs[cslot][:, t, :], in_=xch[cslot][:, t, :], func=AF.Copy, scale=rstd[:, t:t + 1]),
                            reads=[Bxch[cslot], Brstd], writes=[Bxs[cslot]])

                def fe_pe(c):
                    cslot = c % 2
                    s_, is_s_, loc_ = chunk_info(c)
                    mo = (l * NSEQ + s_) * 8
                    for fc in range(8):
                        tp = fc % 2
                        for t in range(4):
                            R.op("pe", lambda e, tp=tp, t=t, fc=fc: e.transpose(
                                ptr[tp][:, t * 128:(t + 1) * 128], xs[cslot][:, t, fc * 128:(fc + 1) * 128], ident),
                                reads=[Bxs[cslot], B_const], writes=[Bptr[tp]])
                        R.op("dve", lambda e, tp=tp, fc=fc: e.tensor_scalar(
                            out=hT[cslot][:, fc, :], in0=ptr[tp][:, 0:512], scalar1=AT[:, mo + fc:mo + fc + 1],
                            scalar2=BT[:, mo + fc:mo + fc + 1], op0=ALU.mult, op1=ALU.add),
                            reads=[Bptr[tp], B_const], writes=[BhT[cslot]])

                emit_xload(0)
                if NCH > 1:
                    emit_xload(1)
                emit_csload(0)
                fe_act(0)
                fe_pe(0)

                def chunk_body(c):
                    s, is_s, loc = chunk_info(c)
                    cslot = c % 2
                    r0 = c * 512
                    if c + 2 < NCH:
                        emit_xload(c + 2)
                    if c + 1 < NCH:
                        emit_csload(c + 1)

                    def next_pm():
                        k = cnt["mm"] % 4
                        cnt["mm"] += 1
                        return k

                    def next_fm():
                        k = cnt["fm"] % NFM
                        cnt["fm"] += 1
                        return k

                    def next_aux():
                        k = cnt["aux"] % 2
                        cnt["aux"] += 1
                        return k

                    def fm_matmul(blk, pmi):
                        for fc in range(8):
                            R.op("pe", lambda e, fc=fc, blk=blk, pmi=pmi, cslot=cslot: e.matmul(
                                pmm[pmi][:, :], lhsT=W_b[:, fc, blk * 128:(blk + 1) * 128], rhs=hT[cslot][:, fc, :],
                                start=(fc == 0), stop=(fc == 7)),
                                reads=[BW, BhT[cslot]], writes=[Bpmm[pmi]])

                    def kdst(kunit):
                        if is_s:
                            return dap(kTc_d[par][kunit], loc * 512, [[TS, 128], [1, 512]]), BkTc
                        return dap(kTp_d, kunit * 128 * TP + loc * 512, [[TP, 128], [1, 512]]), None

                    def store_fm(fi, dst, extra_w=None):
                        ws = [] if extra_w is None else [extra_w]
                        R.dma("sp", lambda e, fi=fi, dst=dst: e.dma_start(out=dst, in_=fmo[fi][:]),
                              f"fmo{fi}", reads=[Bfmo[fi]], writes=ws)

                    def do_qa(h):
                        pmi, fi = next_pm(), next_fm()
                        fm_matmul(h, pmi)
                        R.op("dve", lambda e: e.tensor_scalar_mul(out=fmo[fi][:], in0=pmm[pmi][:, :], scalar1=0.125),
                             reads=[Bpmm[pmi]], writes=[Bfmo[fi]])
                        store_fm(fi, dap(qT_d, h * 128 * T + r0, [[T, 128], [1, 512]]))

                    def do_ka(h):
                        pmi, fi = next_pm(), next_fm()
                        fm_matmul(4 + h, pmi)
                        R.op("act", lambda e: e.activation(out=fmo[fi][:], in_=pmm[pmi][:, :], func=AF.Copy),
                             reads=[Bpmm[pmi]], writes=[Bfmo[fi]])
                        dst, bw = kdst(h)
                        store_fm(fi, dst, bw)

                    def do_g(j):
                        pmi, fi = next_pm(), next_fm()
                        blk = 8 + j if j < 4 else 18 + (j - 4)
                        fm_matmul(blk, pmi)
                        R.op("act", lambda e: e.activation(out=fmo[fi][:], in_=pmm[pmi][:, :], func=AF.Silu),
                             reads=[Bpmm[pmi]], writes=[Bfmo[fi]])
                        store_fm(fi, dap(gT_d, j * 128 * T + r0, [[T, 128], [1, 512]]))

                    def do_v(t):
                        vs = (c * 4 + t) % 2
                        pmi, pmi2 = next_pm(), next_pm()
                        for fc in range(8):
                            R.op("pe", lambda e, fc=fc: e.matmul(
                                pmm[pmi][:, :], lhsT=hT[cslot][:, fc, t * 128:(t + 1) * 128],
                                rhs=W_b[:, fc, NQ7:NQ7 + 512], start=(fc == 0), stop=(fc == 7)),
                                reads=[BW, BhT[cslot]], writes=[Bpmm[pmi]])
                        for fc in range(8):
                            R.op("pe", lambda e, fc=fc: e.matmul(
                                pmm[pmi2][:, 0:128], lhsT=hT[cslot][:, fc, t * 128:(t + 1) * 128],
                                rhs=W_b[:, fc, NQ7 + 512:NQ7 + 640], start=(fc == 0), stop=(fc == 7)),
                                reads=[BW, BhT[cslot]], writes=[Bpmm[pmi2]])
                        R.op("dve", lambda e: e.tensor_copy(out=vout[vs][:, 0:512], in_=pmm[pmi][:, :]),
                             reads=[Bpmm[pmi]], writes=[Bvout[vs]])
                        R.op("dve", lambda e: e.tensor_copy(out=vout[vs][:, 512:640], in_=pmm[pmi2][:, 0:128]),
                             reads=[Bpmm[pmi2]], writes=[Bvout[vs]])
                        if is_s:
                            vdst = dap(vc_d[par][loc], t * 128 * 640, [[640, 128], [1, 640]])
                            R.dma("sp", lambda e: e.dma_start(out=vdst, in_=vout[vs][:]),
                                  f"vout{vs}", reads=[Bvout[vs]], writes=[Bvc])
                        else:
                            vdst = dap(vp_d, (loc * 512 + t * 128) * 640, [[640, 128], [1, 640]])
                            R.dma("sp", lambda e: e.dma_start(out=vdst, in_=vout[vs][:]),
                                  f"vout{vs}", reads=[Bvout[vs]])

                    chain_pm = {}

                    def stageA(j):
                        a = j % 3
                        pmi = next_pm()
                        chain_pm[j] = pmi
                        fm_matmul(12 + j, pmi)
                        R.op("act", lambda e: e.activation(out=sqb[a][:], in_=pmm[pmi][:, :], func=AF.Square),
                             reads=[Bpmm[pmi]], writes=[Bsqb[a]])

                    def stageB(j):
                        a = j % 3
                        pmi = chain_pm[j]
                        ax = next_aux()
                        R.op("pe", lambda e: e.matmul(paux[ax][:, :], lhsT=bones, rhs=sqb[a][:], start=True, stop=True),
                             reads=[Bsqb[a], B_const], writes=[Bpaux[ax]])
                        R.op("act", lambda e: e.activation(out=rs[a][:], in_=paux[ax][:, :], func=AF.Ln,
                                                           scale=1.0 / 64, bias=EPS),
                             reads=[Bpaux[ax]], writes=[Brs[a]])
                        R.op("act", lambda e: e.activation(out=rs[a][:], in_=rs[a][:], func=AF.Exp, scale=-0.5),
                             reads=[Brs[a]], writes=[Brs[a]])
                        gcol = qg2[:, l:l + 1] if j < 4 else smallp[:, 2 * L + l:2 * L + l + 1]
                        R.op("dve", lambda e: e.scalar_tensor_tensor(
                            out=qnf[a][:], in0=pmm[pmi][:, :], scalar=gcol, in1=rs[a][:], op0=ALU.mult, op1=ALU.mult),
                            reads=[Bpmm[pmi], Brs[a], B_const], writes=[Bqnf[a]])
                        R.op("dve", lambda e: e.tensor_copy(out=qnb[a][:], in_=qnf[a][:]),
                             reads=[Bqnf[a]], writes=[Bqnb[a]])

                    def stageC(j):
                        a = j % 3
                        fi = next_fm()
                        ax2 = next_aux()
                        R.op("pe", lambda e: e.matmul(paux[ax2][:, :], lhsT=rrot, rhs=qnb[a][:], start=True, stop=True),
                             reads=[Bqnb[a], B_const], writes=[Bpaux[ax2]])
                        R.op("dve", lambda e: e.tensor_tensor(out=t2[a][:], in0=paux[ax2][:, :], in1=cs[cslot][:, 1, :],
                                                              op=ALU.mult),
                             reads=[Bpaux[ax2], Bcs[cslot]], writes=[Bt2[a]])
                        R.op("dve", lambda e: e.tensor_tensor(out=qnf[a][:], in0=qnf[a][:], in1=cs[cslot][:, 0, :],
                                                              op=ALU.mult),
                             reads=[Bqnf[a], Bqnb[a], Bcs[cslot]], writes=[Bqnf[a]])
                        R.op("dve", lambda e: e.tensor_tensor(out=fmo[fi][:], in0=qnf[a][:], in1=t2[a][:], op=ALU.add),
                             reads=[Bqnf[a], Bt2[a]], writes=[Bfmo[fi]])
                        if j < 4:
                            store_fm(fi, dap(qT_d, (4 + j) * 128 * T + r0, [[T, 128], [1, 512]]))
                        else:
                            dst, bw = kdst(4 + (j - 4))
                            store_fm(fi, dst, bw)

                    fillers = [(1, do_qa, h) for h in range(4)] + [(1, do_ka, h) for h in range(4)] + \
                              [(1, do_g, j) for j in range(8)] + [(2, do_v, t) for t in range(4)]
                    tick = 0
                    while fillers or tick < 8:
                        if 0 <= tick - 1 < 6:
                            stageB(tick - 1)
                        if 0 <= tick - 2 < 6:
                            stageC(tick - 2)
                        budget = 2
                        while fillers and fillers[0][0] <= budget:
                            cost, fn, arg = fillers.pop(0)
                            budget -= cost
                            fn(arg)
                        if tick < 6:
                            stageA(tick)
                        budget = 1
                        while fillers and fillers[0][0] <= budget:
                            cost, fn, arg = fillers.pop(0)
                            budget -= cost
                            fn(arg)
                        if c + 1 < NCH:
                            if tick == 0:
                                fe_act(c + 1)
                            if tick == 4:
                                fe_pe(c + 1)
                        tick += 1
                    if c == NJ - 1 and not cfg.noag:
                        extra = []
                        for nm in [f"fmo{i}" for i in range(NFM)] + ["vout0", "vout1"]:
                            d = R.dsem.get(nm)
                            if d is not None:
                                extra.append((d[0], d[1], "d_" + nm))
                        prev_ev = None
                        pairs = [(kTc_d[par][k], kTg_d[par][k]) for k in range(6)] + \
                                [(vc_d[par][j], vg_d[par][j]) for j in range(NJ)]
                        for (src_t, dst_t) in pairs:
                            ex = list(extra) + ([prev_ev] if prev_ev is not None else [])
                            prev_ev = R.dma("pool", lambda e, src_t=src_t, dst_t=dst_t: e.collective_compute(
                                "AllGather", ALU.bypass, replica_groups=[[0, 1, 2, 3], [4, 5, 6, 7]],
                                ins=[src_t.ap()], outs=[dst_t.ap()]), "cc", extra=ex, inc=1)

                for c in range(NCH):
                    chunk_body(c)
                R.barrier()

        def P3(l, par, last):
            with ExitStack() as st:
                kbuf = [sb(f"kbuf{i}", [128, SMAX], BF16, st) for i in range(2)]
                vbuf = [sb(f"vbuf{i}", [128, KMAX, 128], BF16, st) for i in range(2)]
                Hbuf = [sb(f"Hbuf{i}", [128, 2688], BF16, st) for i in range(2)]
                NQB = 3
                qbuf = [sb(f"qbuf{i}", [128, 512], BF16, st) for i in range(NQB)]
                gbuf = [sb(f"gbuf{i}", [128, 512], BF16, st) for i in range(4)]
                NPT = 4
                pT = [sb(f"pT{i}", [128, 1024], BF16, st) for i in range(NPT)]
                zsum = [sb(f"zsum{i}", [128, 1024], BF16, st) for i in range(2)]
                Bzsum = [Buf("zsum0"), Buf("zsum1")]
                rzp = [sb(f"rzp{i}", [128, 512], F32, st) for i in range(2)]
                rz0f = sb("rz0f", [128, 512], F32, st)
                rz1f = sb("rz1f", [128, 512], F32, st)
                ob = [sb(f"ob{i}", [128, 512], F32, st) for i in range(2)]
                tb = [sb(f"tb{i}", [128, 512], F32, st) for i in range(2)]
                sq = [sb(f"sq{i}", [128, 512], BF16, st) for i in range(2)]
                rsd = [sb(f"rsd{i}", [128, 512], F32, st) for i in range(2)]
                mixo = [sb(f"mixo{i}", [128, 512], BF16, st) for i in range(2)]
                psS = [ps(f"psS{i}", [128, 1024], F32, st) for i in range(2)]
                psO0 = ps("psO0", [128, 512], F32, st)
                psO1 = ps("psO1", [128, 512], F32, st)
                psZ0 = ps("psZ0", [128, 512], F32, st)

                Bkv = [Buf("kv0"), Buf("kv1")]
                Bbt = [Buf("Hbuf0"), Buf("Hbuf1")]
                Bq = [Buf(f"q{i}") for i in range(NQB)]
                Bg = [Buf(f"g{i}") for i in range(4)]
                BpT = [Buf(f"pT{i}") for i in range(NPT)]
                BpsS = [Buf("psS0"), Buf("psS1")]
                BaccO, BaccZ = Buf("accO"), Buf("accZ")
                Brzp, Brzf = [Buf("rzp0"), Buf("rzp1")], Buf("rzf")
                Bob, Btb = [Buf("ob0"), Buf("ob1")], [Buf("tb0"), Buf("tb1")]
                Bsq = [Buf("sq0"), Buf("sq1")]
                Brsd = [Buf("rsd0"), Buf("rsd1")]
                Bmixo = [Buf("mixo0"), Buf("mixo1")]

                items = []
                for i in range(NP):
                    for h in range(4):
                        items.append(("p", i, "A", h))
                    for n in range(2):
                        items.append(("p", i, "B", n))
                for h in range(4):
                    items.append(("s", 0, "A", h))
                for n in range(2):
                    items.append(("s", 0, "B", n))

                def load_item(ii):
                    job, si, kind, idx = items[ii]
                    slot = ii % 2
                    kunit = idx if kind == "A" else 4 + idx
                    vcol = idx * 128 if kind == "A" else 512 + idx * 64
                    vw = 128 if kind == "A" else 64
                    if job == "p":
                        ksrc = dap(kTp_d, kunit * 128 * TP + si * SP, [[TP, 128], [1, SP]])
                        R.dma("sp", lambda e: e.dma_start(out=kbuf[slot][:, 0:SP], in_=ksrc), f"kv{slot}",
                              writes=[Bkv[slot]])
                        for g0 in range(0, KP, 16):
                            gn = min(16, KP - g0)
                            vsrc = dap(vp_d, (si * SP + g0 * 128) * 640 + vcol, [[640, 128], [128 * 640, gn], [1, vw]])
                            R.dma("sp", lambda e, g0=g0, gn=gn, vsrc=vsrc: e.dma_start(
                                out=vbuf[slot][:, g0:g0 + gn, 0:vw], in_=vsrc), f"kv{slot}", writes=[Bkv[slot]])
                    else:
                        kview = kbuf[slot][:, 0:SS].rearrange("p (j r t) -> p j r t", r=4, t=512)
                        for rr in range(4):
                            ksrc = dap(kTg_d[par][kunit], rr * 128 * TS, [[TS, 128], [512, NJ], [1, 512]])
                            R.dma("sp", lambda e, rr=rr, ksrc=ksrc: e.dma_start(
                                out=kview[:, :, rr, :], in_=ksrc), f"kv{slot}", writes=[Bkv[slot]])
                        for j in range(NJ):
                            vsrc = dap(vg_d[par][j], vcol, [[640, 128], [128 * 640, 16], [1, vw]])
                            R.dma("sp", lambda e, j=j, vsrc=vsrc: e.dma_start(
                                out=vbuf[slot][:, j * 16:(j + 1) * 16, 0:vw], in_=vsrc), f"kv{slot}", writes=[Bkv[slot]])

                a_items = [ii for ii, it in enumerate(items) if it[2] == "A"]
                hslot = {ii: k % 2 for k, ii in enumerate(a_items)}

                def load_btiles(ii):
                    job, si, kind, idx = items[ii]
                    if kind != "A":
                        return
                    hs = hslot[ii]
                    if job == "p":
                        src = dap(M_d, idx * LM, [[1, 128], [1, 1152]])
                        R.dma("sp", lambda e: e.dma_start(out=Hbuf[hs][:, 0:1152], in_=src), f"Hbuf{hs}",
                              writes=[Bbt[hs]])
                    else:
                        src = dap(M_d, idx * LM + LP_M, [[1, 128], [1, 2688]])
                        R.dma("sp", lambda e: e.dma_start(out=Hbuf[hs][:, :], in_=src), f"Hbuf{hs}",
                              writes=[Bbt[hs]])

                groups = []
                for ii, (job, si, kind, idx) in enumerate(items):
                    qunits = [idx] if kind == "A" else [4 + 2 * idx, 4 + 2 * idx + 1]
                    nq = QP if job == "p" else NJ
                    for qu in qunits:
                        for qc in range(nq):
                            groups.append((ii, qu, qc))

                def tok0(ii, qc):
                    job, si, kind, idx = items[ii]
                    return TS + si * SP + qc * 512 if job == "p" else qc * 512

                def load_qg(gi):
                    ii, qu, qc = groups[gi]
                    t0 = tok0(ii, qc)
                    qs = gi % NQB
                    gs = gi % 4
                    qsrc = dap(qT_d, qu * 128 * T + t0, [[T, 128], [1, 512]])
                    gsrc = dap(gT_d, qu * 128 * T + t0, [[T, 128], [1, 512]])
                    R.dma("sp", lambda e: e.dma_start(out=qbuf[qs][:], in_=qsrc), f"q{qs}", writes=[Bq[qs]])
                    R.dma("sp", lambda e: e.dma_start(out=gbuf[gs][:], in_=gsrc), f"g{gs}", writes=[Bg[gs]])

                state = {"sidx": 0, "pidx": 0, "eidx": 0, "zidx": 0, "zprev": 0}
                pending = []

                def emit_S(gi, kc):
                    ii, qu, qc = groups[gi]
                    job, si, kind, idx = items[ii]
                    slot = ii % 2
                    qs = gi % NQB
                    ss = state["sidx"] % 2
                    state["sidx"] += 1
                    zone = None
                    if kind == "A":
                        if job == "p":
                            w = kc - 4 * qc
                            lo, hi = -1, 4
                        else:
                            w = kc - 16 * qc
                            lo, hi = -1, 16
                        if w < lo:
                            zone = ("neg", None)
                        elif w > hi:
                            zone = ("pos", None)
                        else:
                            zone = ("near", w + 1)
                    near = zone is not None and zone[0] == "near"
                    kcols = slice(kc * 128, (kc + 1) * 128)
                    hs = hslot.get(ii, 0)
                    rd = [Bkv[slot], Bq[qs]] + ([Bbt[hs], B_const] if near else [])
                    for m in range(2):
                        pr = slice(m * 64, (m + 1) * 64)
                        oc = slice(m * 512, (m + 1) * 512)
                        R.op("pe", lambda e, ss=ss, pr=pr, oc=oc, slot=slot, kcols=kcols, qs=qs, near=near: e.matmul(
                            psS[ss][:, oc], lhsT=kbuf[slot][pr, kcols], rhs=qbuf[qs][pr, :], start=True,
                            stop=(not near)), reads=rd, writes=[BpsS[ss]])
                    if near:
                        w = zone[1] - 1
                        off = (512 - 128 * w) if job == "p" else (2048 - 128 * w)
                        for m in range(2):
                            oc = slice(m * 512, (m + 1) * 512)
                            R.op("pe", lambda e, ss=ss, oc=oc, off=off, hs=hs: e.matmul(
                                psS[ss][:, oc], lhsT=Jm, rhs=Hbuf[hs][:, off:off + 512], start=False, stop=True),
                                reads=rd, writes=[BpsS[ss]])
                    return ss, zone

                def emit_exp(gi, ss, zone):
                    ii, qu, qc = groups[gi]
                    job, si, kind, idx = items[ii]
                    pi = state["pidx"] % NPT
                    state["pidx"] += 1
                    if zone is None or zone[0] == "near":
                        R.op("act", lambda e, ss=ss, pi=pi: e.activation(out=pT[pi][:], in_=psS[ss][:, :], func=AF.Exp),
                             reads=[BpsS[ss]], writes=[BpT[pi]])
                    else:
                        col = idx if zone[0] == "neg" else 4 + idx
                        R.op("act", lambda e, ss=ss, pi=pi, col=col: e.activation(
                            out=pT[pi][:], in_=psS[ss][:, :], func=AF.Exp, bias=far[:, col:col + 1]),
                            reads=[BpsS[ss], Bsmall_dummy], writes=[BpT[pi]])
                    return pi

                def emit_PV(gi, kc, pi, first, lastk):
                    ii, qu, qc = groups[gi]
                    job, si, kind, idx = items[ii]
                    slot = ii % 2
                    rd = [Bkv[slot], BpT[pi], B_const]
                    if kind == "A":
                        for (acc, c0) in ((psO0, 0), (psO1, 512)):
                            R.op("pe", lambda e, acc=acc, c0=c0, slot=slot, kc=kc, pi=pi: e.matmul(
                                acc[:, :], lhsT=vbuf[slot][:, kc, :],
                                rhs=pT[pi][:, c0:c0 + 512], start=first, stop=lastk),
                                reads=rd, writes=[BaccO])
                        if kc % 2 == 0:
                            state["zprev"] = pi
                        else:
                            zi = state["zidx"] % 2
                            state["zidx"] += 1
                            p0 = state["zprev"]
                            R.op("dve", lambda e, zi=zi, p0=p0, pi=pi: e.tensor_tensor(
                                out=zsum[zi][:], in0=pT[p0][:], in1=pT[pi][:], op=ALU.add),
                                reads=[BpT[p0], BpT[pi]], writes=[Bzsum[zi]])
                            for m in range(2):
                                R.op("pe", lambda e, m=m, zi=zi: e.matmul(
                                    psZ0[m * 64:(m + 1) * 64, :], lhsT=ones_b[:, 0:64],
                                    rhs=zsum[zi][:, m * 512:(m + 1) * 512], start=(kc == 1), stop=lastk,
                                    tile_position=(0, m * 64)),
                                    reads=[Bzsum[zi], B_const], writes=[BaccZ])
                    else:
                        for (acc, lhs, bb) in ((psO0, None, BaccO), (psZ0, "ones", BaccZ)):
                            for m in range(2):
                                R.op("pe", lambda e, acc=acc, lhs=lhs, m=m, slot=slot, kc=kc, pi=pi: e.matmul(
                                    acc[m * 64:(m + 1) * 64, :],
                                    lhsT=(ones_b[:, 0:64] if lhs == "ones" else vbuf[slot][:, kc, 0:64]),
                                    rhs=pT[pi][:, m * 512:(m + 1) * 512], start=first, stop=lastk,
                                    tile_position=(0, m * 64)),
                                    reads=rd, writes=[bb])

                def emit_epilogue(gi):
                    ii, qu, qc = groups[gi]
                    job, si, kind, idx = items[ii]
                    t0 = tok0(ii, qc)
                    gs = gi % 4
                    ei = state["eidx"] % 2
                    state["eidx"] += 1
                    mdst = dap(mixT_d, qu * 128 * T + t0, [[T, 128], [1, 512]])
                    if kind == "A":
                        R.op("act", lambda e, ei=ei: e.activation(out=rzp[ei][:], in_=psZ0[:, :], func=AF.Ln),
                             reads=[BaccZ], writes=[Brzp[ei]])
                        R.op("act", lambda e, ei=ei: e.activation(out=rzp[ei][:], in_=rzp[ei][:], func=AF.Exp, scale=-1.0),
                             reads=[Brzp[ei]], writes=[Brzp[ei]])
                    else:
                        R.op("dve", lambda e, ei=ei: e.tensor_copy(out=rzp[ei][:], in_=psZ0[:, :]),
                             reads=[BaccZ], writes=[Brzp[ei]])
                    if kind == "A":
                        R.op("dve", lambda e, ei=ei: e.tensor_copy(out=ob[ei][:], in_=psO0[:, :]),
                             reads=[BaccO], writes=[Bob[ei]])
                        R.op("dve", lambda e, ei=ei: e.tensor_copy(out=tb[ei][:], in_=psO1[:, :]),
                             reads=[BaccO], writes=[Btb[ei]])
                        war = list(Brzf.r.values())
                        R.dma("pool", lambda e, ei=ei: e.dma_start(out=rz0f[0:64, :], in_=rzp[ei][0:64, :]), "rzf",
                              reads=[Brzp[ei]], extra=war)
                        R.dma("pool", lambda e, ei=ei: e.dma_start(out=rz0f[64:128, :], in_=rzp[ei][0:64, :]), "rzf",
                              reads=[Brzp[ei]])
                        R.dma("pool", lambda e, ei=ei: e.dma_start(out=rz1f[0:64, :], in_=rzp[ei][64:128, :]), "rzf",
                              reads=[Brzp[ei]])
                        R.dma("pool", lambda e, ei=ei: e.dma_start(out=rz1f[64:128, :], in_=rzp[ei][64:128, :]), "rzf",
                              reads=[Brzp[ei]], writes=[Brzf])
                        R.op("dve", lambda e, ei=ei: e.tensor_tensor(out=ob[ei][:], in0=ob[ei][:], in1=rz0f[:], op=ALU.mult),
                             reads=[Bob[ei], Brzf], writes=[Bob[ei]])
                        R.op("dve", lambda e, ei=ei: e.tensor_tensor(out=tb[ei][:], in0=tb[ei][:], in1=rz1f[:], op=ALU.mult),
                             reads=[Btb[ei], Brzf], writes=[Btb[ei]])
                        R.op("dve", lambda e, ei=ei: e.scalar_tensor_tensor(
                            out=ob[ei][:], in0=tb[ei][:], scalar=neglam[:, l:l + 1], in1=ob[ei][:], op0=ALU.mult, op1=ALU.add),
                            reads=[Btb[ei], Bob[ei], B_const], writes=[Bob[ei]])
                        R.op("dve", lambda e, ei=ei: e.tensor_tensor(out=sq[ei][:], in0=ob[ei][:], in1=ob[ei][:], op=ALU.mult),
                             reads=[Bob[ei]], writes=[Bsq[ei]])

                        def tail(ei=ei, gs=gs, mdst=mdst):
                            ss = state["sidx"] % 2
                            state["sidx"] += 1
                            R.op("pe", lambda e, ss=ss, ei=ei: e.matmul(psS[ss][:, 0:512], lhsT=ones_b, rhs=sq[ei][:],
                                                                       start=True, stop=True),
                                 reads=[Bsq[ei], B_const], writes=[BpsS[ss]])
                            R.op("act", lambda e, ss=ss, ei=ei: e.activation(
                                out=rsd[ei][:], in_=psS[ss][:, 0:512], func=AF.Ln, scale=1.0 / 128, bias=EPS),
                                reads=[BpsS[ss]], writes=[Brsd[ei]])
                            R.op("act", lambda e, ei=ei: e.activation(out=rsd[ei][:], in_=rsd[ei][:], func=AF.Exp, scale=-0.5),
                                 reads=[Brsd[ei]], writes=[Brsd[ei]])
                            R.op("dve", lambda e, ei=ei: e.scalar_tensor_tensor(
                                out=ob[ei][:], in0=ob[ei][:], scalar=gsub[:, l:l + 1], in1=rsd[ei][:],
                                op0=ALU.mult, op1=ALU.mult), reads=[Bob[ei], Brsd[ei], B_const], writes=[Bob[ei]])
                            R.op("dve", lambda e, ei=ei, gs=gs: e.tensor_tensor(
                                out=mixo[ei][:], in0=ob[ei][:], in1=gbuf[gs][:], op=ALU.mult),
                                reads=[Bob[ei], Bg[gs]], writes=[Bmixo[ei]])
                            R.dma("pool", lambda e, ei=ei, mdst=mdst: e.dma_start(out=mdst, in_=mixo[ei][:]),
                                  f"mixo{ei}", reads=[Bmixo[ei]])
                        pending.append([7, tail])
                    else:
                        R.op("dve", lambda e, ei=ei: e.tensor_copy(out=ob[ei][:], in_=psO0[:, :]),
                             reads=[BaccO], writes=[Bob[ei]])
                        R.op("dve", lambda e, ei=ei: e.reciprocal(out=rzp[ei][:], in_=rzp[ei][:]),
                             reads=[Brzp[ei]], writes=[Brzp[ei]])
                        R.op("dve", lambda e, ei=ei: e.tensor_tensor(out=ob[ei][:], in0=ob[ei][:], in1=rzp[ei][:], op=ALU.mult),
                             reads=[Bob[ei], Brzp[ei]], writes=[Bob[ei]])
                        R.op("dve", lambda e, ei=ei, gs=gs: e.tensor_tensor(
                            out=mixo[ei][:], in0=ob[ei][:], in1=gbuf[gs][:], op=ALU.mult),
                            reads=[Bob[ei], Bg[gs]], writes=[Bmixo[ei]])
                        R.dma("pool", lambda e, ei=ei, mdst=mdst: e.dma_start(out=mdst, in_=mixo[ei][:]),
                              f"mixo{ei}", reads=[Bmixo[ei]])

                def tick_pending(force=False):
                    for p in list(pending):
                        p[0] -= 1
                        if force or p[0] <= 0:
                            pending.remove(p)
                            p[1]()

                Bsmall_dummy = B_const
                load_item(0)
                for _ii in a_items[:2]:
                    load_btiles(_ii)
                if len(items) > 1:
                    load_item(1)
                load_qg(0)
                if len(groups) > 1:
                    load_qg(1)
                flat = []
                for gi, (ii, qu, qc) in enumerate(groups):
                    nk = KP if items[ii][0] == "p" else KS
                    for kc in range(nk):
                        flat.append((gi, kc, kc == 0, kc == nk - 1))
                hist = []

                def flush_pv(next_gi):
                    pgi, pkc, ppi, pfirst, plast = hist.pop(0)
                    emit_PV(pgi, pkc, ppi, pfirst, plast)
                    if plast:
                        emit_epilogue(pgi)
                        pii = groups[pgi][0]
                        if next_gi is not None and groups[next_gi][0] != pii:
                            if pii + 2 < len(items):
                                load_item(pii + 2)
                    tick_pending()

                for fi, (gi, kc, first, lastk) in enumerate(flat):
                    ii = groups[gi][0]
                    if first:
                        if gi + 2 < len(groups):
                            load_qg(gi + 2)
                    ss, zone = emit_S(gi, kc)
                    pi = emit_exp(gi, ss, zone)
                    hist.append((gi, kc, pi, first, lastk))
                    if len(hist) > 2:
                        nxt = hist[0][0]
                        flush_pv(hist[1][0])
                    if lastk and (gi + 1 < len(groups)) and groups[gi + 1][0] != ii and ii in hslot:
                        k = a_items.index(ii)
                        if k + 2 < len(a_items):
                            load_btiles(a_items[k + 2])
                while hist:
                    flush_pv(hist[1][0] if len(hist) > 1 else None)
                tick_pending(force=True)
                R.barrier()

        def P4(l, par, last):
            with ExitStack() as st:
                Wo_b = sb("Wo_b", [128, 8, D], BF16, st)
                wos = [sb(f"wos{i}", [128, D], F32, st) for i in range(2)]
                mixin = [sb(f"mixin{i}", [128, 8, 512], BF16, st) for i in range(2)]
                xch = [sb(f"xch4_{i}", [128, 4, D], F32, st) for i in range(2)]
                gate = sb("gate_s", [128, NSEQ, D], F32, st)
                yb = [sb(f"yb{i}", [128, D], F32, st) for i in range(2)]
                xo = [sb(f"xo{i}", [128, D], F32, st) for i in range(2)]
                pmo = [ps(f"pmo{i}", [128, 512], F32, st) for i in range(4)]
                BWo = Buf("Wo")
                Bwos = [Buf("wos0"), Buf("wos1")]
                Bmixin = [Buf("mixin0"), Buf("mixin1")]
                Bxch = [Buf("xch40"), Buf("xch41")]
                Bgate = Buf("gate")
                Byb = [Buf("yb0"), Buf("yb1")]
                Bxo = [Buf("xo0"), Buf("xo1")]
                Bpmo = [Buf(f"pmo{i}") for i in range(4)]
                if last:
                    fg = sb("fg", [128, D], F32, st)
                    junk4 = sb("junk4", [128, D], F32, st)
                    ssq4 = sb("ssq4", [128, 2], F32, st)
                    rst4 = sb("rst4", [128, 2], F32, st)
                    yo = [sb(f"yo{i}", [128, D], F32, st) for i in range(2)]
                    Bfg, Bjunk4 = Buf("fg"), Buf("junk4")
                    Bssq4 = [Buf("ssq40"), Buf("ssq41")]
                    Brst4 = [Buf("rst40"), Buf("rst41")]
                    Byo = [Buf("yo0"), Buf("yo1")]
                    R.dma("sp", lambda e: e.dma_start(out=fg[:], in_=fg_d.ap()), "fg", writes=[Bfg])
                for mc in range(8):
                    slot = mc % 2
                    src = dap(w_out, l * D * D + mc * 128 * D, [[D, 128], [1, D]])
                    R.dma("sp", lambda e, slot=slot, src=src: e.dma_start(out=wos[slot][:], in_=src), f"wos{slot}",
                          writes=[Bwos[slot]])
                    R.op("dve" if mc % 2 == 0 else "pool", lambda e, slot=slot, mc=mc: e.tensor_copy(
                        out=Wo_b[:, mc, :], in_=wos[slot][:]), reads=[Bwos[slot]], writes=[BWo])
                gsrc = dap(gate_d, l * NSEQ * 128 * D, [[D, 128], [128 * D, NSEQ], [1, D]])
                R.dma("sp", lambda e: e.dma_start(out=gate[:], in_=gsrc), "gate", writes=[Bgate])
                mk = 0
                tk = 0
                for c in range(NCH):
                    s, is_s, loc = chunk_info(c)
                    cslot = c % 2
                    r0 = c * 512
                    msrc = dap(mixT_d, r0, [[T, 128], [128 * T, 8], [1, 512]])
                    R.dma("sp", lambda e, cslot=cslot, msrc=msrc: e.dma_start(out=mixin[cslot][:], in_=msrc),
                          f"mixin{cslot}", writes=[Bmixin[cslot]])
                    if l == 0:
                        xsrc = dap(x_s, loc * 512 * D, [[D, 128], [128 * D, 4], [1, D]]) if is_s else \
                            dap(x_p, loc * 512 * D, [[D, 128], [128 * D, 4], [1, D]])
                    else:
                        xsrc = dap(xbuf, r0 * D, [[D, 128], [128 * D, 4], [1, D]])
                    R.dma("sp", lambda e, cslot=cslot, xsrc=xsrc: e.dma_start(out=xch[cslot][:], in_=xsrc),
                          f"xch4_{cslot}", writes=[Bxch[cslot]])
                    for t in range(4):
                        ts_ = tk % 2
                        tk += 1
                        for half in range(2):
                            pmi = mk % 4
                            mk += 1
                            for mc in range(8):
                                R.op("pe", lambda e, pmi=pmi, mc=mc, t=t, half=half, cslot=cslot: e.matmul(
                                    pmo[pmi][:, :], lhsT=mixin[cslot][:, mc, t * 128:(t + 1) * 128],
                                    rhs=Wo_b[:, mc, half * 512:(half + 1) * 512], start=(mc == 0), stop=(mc == 7)),
                                    reads=[Bmixin[cslot], BWo], writes=[Bpmo[pmi]])
                            R.op("dve", lambda e, pmi=pmi, half=half, ts_=ts_, s=s: e.tensor_tensor(
                                out=yb[ts_][:, half * 512:(half + 1) * 512], in0=pmo[pmi][:, :],
                                in1=gate[:, s, half * 512:(half + 1) * 512], op=ALU.mult),
                                reads=[Bpmo[pmi], Bgate], writes=[Byb[ts_]])
                        R.op("pool", lambda e, ts_=ts_, cslot=cslot, t=t: e.tensor_tensor(
                            out=xo[ts_][:], in0=xch[cslot][:, t, :], in1=yb[ts_][:], op=ALU.add),
                            reads=[Bxch[cslot], Byb[ts_]], writes=[Bxo[ts_]])
                        if not last:
                            xdst = dap(xbuf, (r0 + t * 128) * D, [[D, 128], [1, D]])
                            R.dma("pool", lambda e, ts_=ts_, xdst=xdst: e.dma_start(out=xdst, in_=xo[ts_][:]),
                                  f"xo{ts_}", reads=[Bxo[ts_]])
                        else:
                            R.op("act", lambda e, ts_=ts_: e.activation(
                                out=junk4[:], in_=xo[ts_][:], func=AF.Square, accum_out=ssq4[:, ts_:ts_ + 1]),
                                reads=[Bxo[ts_]], writes=[Bjunk4, Bssq4[ts_]])
                            R.op("act", lambda e, ts_=ts_: e.activation(
                                out=rst4[:, ts_:ts_ + 1], in_=ssq4[:, ts_:ts_ + 1], func=AF.Ln, scale=1.0 / D, bias=EPS),
                                reads=[Bssq4[ts_]], writes=[Brst4[ts_]])
                            R.op("act", lambda e, ts_=ts_: e.activation(
                                out=rst4[:, ts_:ts_ + 1], in_=rst4[:, ts_:ts_ + 1], func=AF.Exp, scale=-0.5),
                                reads=[Brst4[ts_]], writes=[Brst4[ts_]])
                            R.op("dve", lambda e, ts_=ts_: e.scalar_tensor_tensor(
                                out=yo[ts_][:], in0=xo[ts_][:], scalar=rst4[:, ts_:ts_ + 1], in1=fg[:],
                                op0=ALU.mult, op1=ALU.mult), reads=[Bxo[ts_], Brst4[ts_], Bfg], writes=[Byo[ts_]])
                            if is_s:
                                ydst = dap(y_s, (loc * 512 + t * 128) * D, [[D, 128], [1, D]])
                            else:
                                ydst = dap(y_p, (loc * 512 + t * 128) * D, [[D, 128], [1, D]])
                            R.dma("pool", lambda e, ts_=ts_, ydst=ydst: e.dma_start(out=ydst, in_=yo[ts_][:]),
                                  f"yo{ts_}", reads=[Byo[ts_]])
                R.barrier()

        for l in range(L):
            par = l % 2
            last = (l == L - 1)
            if cfg.stop < 1:
                break
            P1(l, par, last)
            if cfg.stop < 3:
                break
            P3(l, par, last)
            if cfg.stop < 4:
                break
            P4(l, par, last)

        with nc.Block() as block:
            @block.tensor
            def _(e):
                R.replay("pe", e)

            @block.scalar
            def _(e):
                R.replay("act", e)

            @block.vector
            def _(e):
                R.replay("dve", e)

            @block.gpsimd
            def _(e):
                R.replay("pool", e)

            @block.sync
            def _(e):
                R.replay("sp", e)
    return nc


def t5_bucket_np(rel):
    rel = np.asarray(rel, dtype=np.int64)
    half, max_exact = 16, 8
    ret = np.where(rel > 0, half, 0)
    n = np.abs(rel)
    nf = np.maximum(n, 1).astype(np.float32)
    large = max_exact + (np.log(nf / np.float32(max_exact)) / np.float32(math.log(128 / max_exact))
                         * np.float32(half - max_exact)).astype(np.int32)
    large = np.minimum(large, half - 1)
    return ret + np.where(n < max_exact, n, large)


def rope_tables(pos):
    pos = np.asarray(pos)
    row = (pos // 64).astype(np.float32)
    col = (pos % 64).astype(np.float32)
    inv = (np.float32(10000.0) ** (-np.arange(0, 32, 2, dtype=np.float32) / np.float32(32))).astype(np.float32)
    ang_r = row[:, None] * inv[None, :]
    ang_c = col[:, None] * inv[None, :]
    ang = np.concatenate([ang_r, ang_r, ang_c, ang_c], axis=-1).astype(np.float32)
    return np.cos(ang).T.astype(np.float32), np.sin(ang).T.astype(np.float32)


def make_consts():
    c = np.zeros((128, 640), np.float32)
    c[:, 0:128] = np.eye(128)
    c[:, 128:256] = np.eye(128)[::-1]
    c[:, 256:384] = 1.0
    c[0:64, 384:448] = 1.0
    c[64:128, 448:512] = 1.0
    Rm = np.zeros((64, 64), np.float32)
    for d in range(64):
        q = d // 16
        if q % 2 == 0:
            Rm[d + 16, d] = -1.0
        else:
            Rm[d - 16, d] = 1.0
    c[0:64, 512:576] = Rm
    c[64:128, 576:640] = Rm
    return c


def make_core_inputs(cfg, core, inp):
    L, NP, SP, NJ = cfg.L, cfg.NP, cfg.SP, cfg.NJ
    NSEQ = 1 + NP
    TS, TP = NJ * 512, NP * SP
    T = TS + TP
    sseq, rank = core // 4, core % 4
    xs_full = inp["x_sample"][sseq]
    chunks = [xs_full[(4 * j + rank) * 512:(4 * j + rank + 1) * 512] for j in range(NJ)]
    x_s = np.ascontiguousarray(np.concatenate(chunks, 0))
    x_p = np.ascontiguousarray(inp["x_prompt"][core * NP:(core + 1) * NP].reshape(TP, D))
    cs_ = np.concatenate([inp["c_sample"][sseq:sseq + 1], inp["c_prompt"][core * NP:(core + 1) * NP]], 0)
    cT = np.ascontiguousarray(cs_.reshape(NSEQ, 8, 128).transpose(2, 1, 0).reshape(128, 8 * NSEQ))
    b_ada = inp["b_ada"]
    b_adaT = np.ascontiguousarray(b_ada[:, :2048].reshape(L, 16, 128).transpose(2, 0, 1).reshape(128, L * 16))
    b_gate = np.ascontiguousarray(b_ada[:, 2048:3072])
    ngT = np.ascontiguousarray(inp["norm_g"].reshape(L, 8, 128).transpose(2, 0, 1).reshape(128, L * 8))
    smallp = np.zeros((128, 3 * L + 8), np.float32)
    smallp[:, 0:L] = inp["subln_g"].T
    smallp[:, L:2 * L] = np.concatenate([inp["q_norm_g"].T, inp["q_norm_g"].T], 0)
    smallp[:, 2 * L:3 * L] = np.concatenate([inp["k_norm_g"].T, inp["k_norm_g"].T], 0)
    smallp[:, 3 * L:3 * L + 4] = inp["rel_table"][15][None, :]
    smallp[:, 3 * L + 4:3 * L + 8] = inp["rel_table"][31][None, :]
    lamrow = np.concatenate([np.concatenate([inp["lam_q1"][l], inp["lam_k1"][l], inp["lam_q2"][l], inp["lam_k2"][l]])
                             for l in range(L)])
    lamv = np.ascontiguousarray(np.broadcast_to(lamrow[None, :], (128, L * 256))).astype(np.float32)
    fgbc = np.ascontiguousarray(np.broadcast_to(inp["final_g"][None, :], (128, D))).astype(np.float32)
    oh = np.zeros((32, LM), np.float32)
    n = np.arange(LP_M)
    oh[t5_bucket_np(639 - n), n] = 1.0
    n = np.arange(LS_M)
    oh[t5_bucket_np(2175 - 512 * rank - n), LP_M + n] = 1.0
    pos_s = np.concatenate([np.arange((4 * j + rank) * 512, (4 * j + rank + 1) * 512) for j in range(NJ)])
    pos = np.concatenate([pos_s] + [np.arange(SP)] * NP)
    cos, sin = rope_tables(pos)
    cstab = np.zeros((128, 2, T), np.float32)
    cstab[0:64, 0], cstab[64:128, 0] = cos, cos
    cstab[0:64, 1], cstab[64:128, 1] = sin, sin
    return {
        "x_s": x_s, "x_p": x_p, "cT": cT, "w_ada": inp["w_ada"], "b_adaT": b_adaT, "b_gate": b_gate,
        "w_in": inp["w_in"], "w_out": inp["w_out"], "ngT": ngT, "smallp": smallp, "lamv": lamv, "fgbc": fgbc,
        "table": np.ascontiguousarray(inp["rel_table"]), "onehot": oh, "cstab": cstab, "consts": make_consts(),
    }


def run_cfg(cfg, inp):
    nc = build_program(cfg)
    in_maps = [make_core_inputs(cfg, core, inp) for core in range(8)]
    res = run_bass_kernel_spmd(nc, in_maps, core_ids=list(range(8)))
    if cfg.debug:
        return res
    L, NP, SP, NJ = cfg.L, cfg.NP, cfg.SP, cfg.NJ
    SS = 4 * NJ * 512
    y_p = np.zeros((8 * NP, SP, D), np.float32)
    y_s = np.zeros((2, SS, D), np.float32)
    for core in range(8):
        r = res.results[core]
        y_p[core * NP:(core + 1) * NP] = np.asarray(r["y_p"]).reshape(NP, SP, D)
        sseq, rank = core // 4, core % 4
        ys = np.asarray(r["y_s"])
        for j in range(NJ):
            y_s[sseq, (4 * j + rank) * 512:(4 * j + rank + 1) * 512] = ys[j * 512:(j + 1) * 512]
    return y_p, y_s


def kernel(x_prompt, x_sample, c_prompt, c_sample, rel_table, norm_g, w_ada, b_ada, w_in,
           lam_q1, lam_k1, lam_q2, lam_k2, subln_g, q_norm_g, k_norm_g, w_out, final_g):
    f = lambda a: np.ascontiguousarray(np.asarray(a, dtype=np.float32))
    inp = dict(x_prompt=f(x_prompt), x_sample=f(x_sample), c_prompt=f(c_prompt), c_sample=f(c_sample),
               rel_table=f(rel_table), norm_g=f(norm_g), w_ada=f(w_ada), b_ada=f(b_ada), w_in=f(w_in),
               lam_q1=f(lam_q1), lam_k1=f(lam_k1), lam_q2=f(lam_q2), lam_k2=f(lam_k2), subln_g=f(subln_g),
               q_norm_g=f(q_norm_g), k_norm_g=f(k_norm_g), w_out=f(w_out), final_g=f(final_g))
    cfg = Cfg(L=4, NP=4, SP=2048, NJ=8)
    y_p, y_s = run_cfg(cfg, inp)
    return (y_p, y_s)
```
